# Optimizing a Trainium2 kernel written in Bass

```python
import math
import jax, jax.numpy as jnp
from jax import lax
import numpy as np

D_MODEL = 1024
BATCH = 32
SEQ = 256
DEPTH = 2
DEC_BATCH = 4
DEC_SEQ = 1024
PAST_LEN = 512

GRID_W = 64
D_MIX = D_MODEL
D_RWKV = D_MIX // 2
HEAD_A = 64
N_HEADS_A = D_RWKV // HEAD_A
D_FNET = D_MIX // 4
N_FNET_GROUPS = 4
FNET_GW = D_FNET // N_FNET_GROUPS
D_POOL = D_MIX - D_RWKV - D_FNET
POOL_WINDOWS = (2, 4, 8, 16)
POOL_GW = D_POOL // len(POOL_WINDOWS)
DECAY_LORA = 64
AAA_LORA = 64
GATE_LORA = 128
N_DIR = 2
D_FF = 4 * D_MODEL
RMS_EPS = 1e-6
GN_EPS = 64e-5

R_OFF = 0
K_OFF = R_OFF + D_RWKV
V_OFF = K_OFF + D_RWKV
WD_OFF = V_OFF + D_RWKV
AD_OFF = WD_OFF + N_DIR * DECAY_LORA
GD_OFF = AD_OFF + N_DIR * AAA_LORA
SHIFT_END = GD_OFF + GATE_LORA
F_OFF = SHIFT_END
P_OFF = F_OFF + D_FNET
D_IN = P_OFF + D_POOL

kernel_name = "hybrid_rwkv7_fnet_pool_prefix_dit_step"

F32 = jnp.float32


def rms_norm(x, g):
    xf = x.astype(F32)
    y = xf * lax.rsqrt(jnp.mean(xf * xf, axis=-1, keepdims=True) + RMS_EPS)
    return (y * g.astype(F32)).astype(x.dtype)


def centred_shift(p, mu):
    zero = jnp.zeros_like(p[:, :1])
    prev = jnp.concatenate([zero, p[:, :-1]], axis=1)
    nxt = jnp.concatenate([p[:, 1:], zero], axis=1)
    return p + mu * (0.5 * (prev + nxt) - p)


def window_mean(x, w, axis):
    n = x.shape[axis]
    cs = jnp.cumsum(x.astype(F32), axis=axis)
    pad = [(0, 0)] * x.ndim
    pad[axis] = (1, 0)
    cs = jnp.pad(cs, pad)
    t = jnp.arange(n)
    lo = jnp.clip(t - w // 2, 0, n)
    hi = jnp.clip(t + w - w // 2, 0, n)
    s = jnp.take(cs, hi, axis=axis) - jnp.take(cs, lo, axis=axis)
    shape = [1] * x.ndim
    shape[axis] = n
    return s / (hi - lo).astype(F32).reshape(shape)


def pool_mixer(p, w_pool, pool_scale, grid):
    B, L, _ = p.shape
    pg = p.reshape(B, L, len(POOL_WINDOWS), POOL_GW)
    means = []
    for i, w in enumerate(POOL_WINDOWS):
        xi = pg[:, :, i]
        if grid:
            rows = L // GRID_W
            xi2 = xi.reshape(B, rows, GRID_W, POOL_GW)
            m = window_mean(window_mean(xi2, w, 1), w, 2).reshape(B, L, POOL_GW)
        else:
            m = window_mean(xi, w, 1)
        means.append(m)
    d = (jnp.stack(means, axis=2) - pg.astype(F32)).astype(p.dtype)
    y = jnp.einsum('blgc,gcd->blgd', d, w_pool).reshape(B, L, D_POOL)
    return y * pool_scale


def fourier_mixer(f, w_fnet):
    B, L, _ = f.shape
    fg = f.reshape(B, L, N_FNET_GROUPS, FNET_GW).astype(F32)
    z = jnp.fft.fft2(fg, axes=(1, 3)).real * (1.0 / math.sqrt(L * FNET_GW))
    y = jnp.einsum('blgc,gcd->blgd', z.astype(f.dtype), w_fnet)
    return y.reshape(B, L, D_FNET)


def wkv_step(S, inp):
    r, w, k, v, a, b = inp
    sa = jnp.einsum('dbhvk,dbhk->dbhv', S, a)
    S = S * w[..., None, :] + sa[..., :, None] * b[..., None, :] + v[..., :, None] * k[..., None, :]
    y = jnp.einsum('dbhvk,dbhk->dbhv', S, r)
    return S, y


def rwkv_mixer(u, s0, w0, w_up, a0, a_up, g_up, k_k, k_a, r_k, ln_w, ln_b):
    B, L, _ = u.shape
    H, N = N_HEADS_A, HEAD_A
    uf = u.astype(F32)
    r = uf[..., R_OFF:R_OFF + D_RWKV]
    k = uf[..., K_OFF:K_OFF + D_RWKV]
    v = uf[..., V_OFF:V_OFF + D_RWKV]
    wd = uf[..., WD_OFF:WD_OFF + N_DIR * DECAY_LORA].reshape(B, L, N_DIR, DECAY_LORA)
    ad = uf[..., AD_OFF:AD_OFF + N_DIR * AAA_LORA].reshape(B, L, N_DIR, AAA_LORA)
    gd = uf[..., GD_OFF:GD_OFF + GATE_LORA]
    w_log = -jax.nn.softplus(-(w0.astype(F32) + jnp.einsum('bldr,drc->bldc', jnp.tanh(wd), w_up.astype(F32)))) - 0.5
    decay = jnp.exp(-jnp.exp(w_log))
    a = jax.nn.sigmoid(a0.astype(F32) + jnp.einsum('bldr,drc->bldc', ad, a_up.astype(F32)))
    g = jnp.einsum('blr,rc->blc', jax.nn.sigmoid(gd), g_up.astype(F32))
    kk = (k * k_k.astype(F32)).reshape(B, L, H, N)
    kk = kk / jnp.maximum(jnp.sqrt(jnp.sum(kk * kk, axis=-1, keepdims=True)), 1e-12)
    kd = k[:, :, None] * (1.0 + (a - 1.0) * k_a.astype(F32))
    a_h = a.reshape(B, L, N_DIR, H, N)
    kd_h = kd.reshape(B, L, N_DIR, H, N)
    dec_h = decay.reshape(B, L, N_DIR, H, N)
    r_h = r.reshape(B, L, H, N)
    v_h = v.reshape(B, L, H, N)
    b_h = kk[:, :, None] * a_h

    def both(z):
        return jnp.broadcast_to(z[:, :, None], (B, L, N_DIR, H, N))

    def time_major(z):
        z = jnp.stack([z[:, :, 0], z[:, ::-1, 1]], axis=0)
        return jnp.transpose(z, (2, 0, 1, 3, 4))

    xs = (time_major(both(r_h)), time_major(dec_h), time_major(kd_h),
          time_major(both(v_h)), time_major(both(-kk)), time_major(b_h))
    s_init = jnp.transpose(s0.astype(F32), (1, 0, 2, 3, 4))
    s_fin, ys = lax.scan(wkv_step, s_init, xs)
    ys = jnp.transpose(ys, (1, 2, 0, 3, 4))
    y = ys[0] + ys[1][:, ::-1]
    mean = jnp.mean(y, axis=-1, keepdims=True)
    var = jnp.mean(jnp.square(y - mean), axis=-1, keepdims=True)
    y = ((y - mean) * lax.rsqrt(var + GN_EPS)).reshape(B, L, D_RWKV)
    y = y * ln_w.astype(F32) + ln_b.astype(F32)
    bonus = jnp.einsum('blhn,bldhn,hn->blh', r_h, kd_h, r_k.astype(F32))[..., None] * v_h
    out = (y + bonus.reshape(B, L, D_RWKV)) * g
    return out.astype(u.dtype), jnp.transpose(s_fin, (1, 0, 2, 3, 4))


def trunk_layer(x, mod, s0, grid, norm1_g, w_in, mu_shift, w0, w_up, a0, a_up, g_up, k_k, k_a, r_k,
                ln_x_w, ln_x_b, w_fnet, w_pool, pool_scale, w_out, norm2_g, w_ff1, w_ff2):
    shift1, scale1, gate1, shift2, scale2, gate2 = jnp.split(mod, 6, axis=-1)
    h = rms_norm(x, norm1_g) * (1 + scale1) + shift1
    proj = jnp.einsum('bld,de->ble', h, w_in)
    u = centred_shift(proj[..., :SHIFT_END], mu_shift)
    y_a, s_fin = rwkv_mixer(u, s0, w0, w_up, a0, a_up, g_up, k_k, k_a, r_k, ln_x_w, ln_x_b)
    y_b = fourier_mixer(proj[..., F_OFF:F_OFF + D_FNET], w_fnet)
    y_c = pool_mixer(proj[..., P_OFF:P_OFF + D_POOL], w_pool, pool_scale, grid)
    mixed = jnp.concatenate([y_a, y_b, y_c], axis=-1)
    x = x + gate1 * jnp.einsum('blc,cd->bld', mixed, w_out)
    h = rms_norm(x, norm2_g) * (1 + scale2) + shift2
    ff = jnp.square(jax.nn.relu(jnp.einsum('bld,df->blf', h, w_ff1)))
    x = x + gate2 * jnp.einsum('blf,fd->bld', ff, w_ff2)
    return x, s_fin


def setup_inputs(seed: int = 0) -> dict:
    key = jax.random.key(seed)
    ks = jax.random.split(key, 32)
    nrm = lambda k, shape, s: jax.random.normal(k, shape, F32) * s
    D = D_MODEL
    return {
        "x_prompt": nrm(ks[0], (BATCH, SEQ, D), 1.0),
        "x_sample": nrm(ks[1], (DEC_BATCH, DEC_SEQ, D), 1.0),
        "state_wkv": nrm(ks[2], (DEC_BATCH, DEPTH, N_DIR, N_HEADS_A, HEAD_A, HEAD_A), 0.5),
        "c": nrm(ks[3], (DEC_BATCH, D), 1.0),
        "c_ctx": nrm(ks[4], (D,), 1.0),
        "w_ada": nrm(ks[5], (DEPTH, D, 6 * D), 0.5 * D ** -0.5),
        "b_ada": nrm(ks[6], (DEPTH, 6 * D), 0.02),
        "norm1_g": 1.0 + nrm(ks[7], (DEPTH, D), 0.05),
        "w_in": nrm(ks[8], (DEPTH, D, D_IN), D ** -0.5),
        "mu_shift": jax.random.uniform(ks[9], (DEPTH, SHIFT_END), F32),
        "w0": jax.random.uniform(ks[10], (DEPTH, N_DIR, D_RWKV), F32, -6.0, -1.0),
        "w_up": nrm(ks[11], (DEPTH, N_DIR, DECAY_LORA, D_RWKV), 0.1 * DECAY_LORA ** -0.5),
        "a0": nrm(ks[12], (DEPTH, N_DIR, D_RWKV), 0.1),
        "a_up": nrm(ks[13], (DEPTH, N_DIR, AAA_LORA, D_RWKV), 0.1 * AAA_LORA ** -0.5),
        "g_up": nrm(ks[14], (DEPTH, GATE_LORA, D_RWKV), GATE_LORA ** -0.5),
        "k_k": 0.85 + nrm(ks[15], (DEPTH, D_RWKV), 0.05),
        "k_a": 1.0 + nrm(ks[16], (DEPTH, D_RWKV), 0.05),
        "r_k": nrm(ks[17], (DEPTH, N_HEADS_A, HEAD_A), 0.1),
        "ln_x_w": 1.0 + nrm(ks[18], (DEPTH, D_RWKV), 0.05),
        "ln_x_b": nrm(ks[19], (DEPTH, D_RWKV), 0.02),
        "w_fnet": nrm(ks[20], (DEPTH, N_FNET_GROUPS, FNET_GW, FNET_GW), FNET_GW ** -0.5),
        "w_pool": nrm(ks[21], (DEPTH, len(POOL_WINDOWS), POOL_GW, POOL_GW), POOL_GW ** -0.5),
        "pool_scale": 1.0 + nrm(ks[22], (DEPTH, D_POOL), 0.05),
        "w_out": nrm(ks[23], (DEPTH, D_MIX, D), D_MIX ** -0.5),
        "norm2_g": 1.0 + nrm(ks[24], (DEPTH, D), 0.05),
        "w_ff1": nrm(ks[25], (DEPTH, D, D_FF), D ** -0.5),
        "w_ff2": nrm(ks[26], (DEPTH, D_FF, D), D_FF ** -0.5),
        "final_norm_g": 1.0 + nrm(ks[27], (D,), 0.05),
    }


def reference(x_prompt, x_sample, state_wkv, c, c_ctx, w_ada, b_ada, norm1_g, w_in, mu_shift, w0, w_up,
              a0, a_up, g_up, k_k, k_a, r_k, ln_x_w, ln_x_b, w_fnet, w_pool, pool_scale, w_out, norm2_g,
              w_ff1, w_ff2, final_norm_g):
    xp = x_prompt
    xs = x_sample
    s_zero = jnp.zeros((xp.shape[0], N_DIR, N_HEADS_A, HEAD_A, HEAD_A), F32)
    new_states = []
    for l in range(DEPTH):
        lw = (norm1_g[l], w_in[l], mu_shift[l], w0[l], w_up[l], a0[l], a_up[l], g_up[l], k_k[l], k_a[l],
              r_k[l], ln_x_w[l], ln_x_b[l], w_fnet[l], w_pool[l], pool_scale[l], w_out[l], norm2_g[l],
              w_ff1[l], w_ff2[l])
        mod_ctx = (jax.nn.silu(c_ctx) @ w_ada[l] + b_ada[l])[None, None, :]
        mod_lat = (jax.nn.silu(c) @ w_ada[l] + b_ada[l])[:, None, :]
        xp, s_ctx = trunk_layer(xp, mod_ctx, s_zero, False, *lw)
        new_states.append(s_ctx)
        xs, _ = trunk_layer(xs, mod_lat, state_wkv[:, l], True, *lw)
    new_state_wkv = jnp.stack(new_states, axis=1).astype(x_prompt.dtype)
    y_prompt = rms_norm(xp, final_norm_g)
    y_sample = rms_norm(xs, final_norm_g)
    return (y_prompt, y_sample, new_state_wkv)
```

```python
import math
from contextlib import ExitStack

import numpy as np
import ml_dtypes
import concourse.bass as bass
import concourse.mybir as mybir
from concourse.bass_utils import run_bass_kernel_spmd

F32 = mybir.dt.float32
BF16 = mybir.dt.bfloat16
AF = mybir.ActivationFunctionType
ALU = mybir.AluOpType
AX = mybir.AxisListType

D = 1024
T = 1536
NSEG = 6
NCH = 24
DEPTH = 2
D_IN = 2432
C0 = math.exp(-0.5)
ENGS = ("pe", "act", "dve", "pool", "sp")
SEG = 2000
GR = 128

PP = {}
_o = 0
for _n, _w in (("n1g", 8), ("mu", 15), ("w0", 8), ("a0", 8), ("kk", 4), ("ka", 4), ("rk", 4), ("lnw", 4),
               ("lnb", 4), ("psc", 2), ("n2g", 8), ("bada", 48), ("fng", 8)):
    PP[_n] = _o
    _o += _w
NPP = _o
DP = {}
_o = 0
for _n, _w in (("omu", 15), ("hmu", 15), ("hw0", 8), ("ha0", 8), ("omka", 4)):
    DP[_n] = _o
    _o += _w
NDP = _o

CF = {}
_o = 0
for _n, _w in (("ident", 128), ("blk", 128), ("ones", 128), ("mSU", 64), ("mIU", 64), ("mSL", 64), ("mIL", 64),
               ("idh", 64), ("csm", 256), ("eps", 4),
               ("mU0", 64), ("mU1", 64), ("mU2", 64), ("mU3", 64), ("mU4", 64), ("mU5", 64),
               ("mL0", 64), ("mL1", 64), ("mL2", 64), ("mL3", 64), ("mL4", 64), ("mL5", 64)):
    CF[_n] = _o
    _o += _w
NCF = _o


class Sched:
    def __init__(self, same_engine_sync=False):
        self.q = {e: [] for e in ENGS}
        self.cnt = {e: 0 for e in ENGS}
        self.last_w = {}
        self.readers = {}
        self.waited = {e: {} for e in ENGS}
        self.dma_tot = {}
        self.same = same_engine_sync

    def _deps(self, eng, reads, writes):
        need = {}

        def nd(tok):
            if tok is None:
                return
            k, v = tok
            if k[0] == "dma":
                v = self.dma_tot[k[1]]
            if need.get(k, 0) < v:
                need[k] = v

        for k in reads:
            nd(self.last_w.get(k))
        for k in writes:
            nd(self.last_w.get(k))
            rd = self.readers.get(k)
            if rd:
                for kk, v in rd.items():
                    nd((kk, v))
        waits = []
        for k, v in need.items():
            if k[0] == "eng" and k[1] == eng and (not self.same or eng in ("pe", "sp")):
                continue
            if self.waited[eng].get(k, 0) >= v:
                continue
            self.waited[eng][k] = v
            waits.append((k, v))
        return waits

    def _record(self, tok, reads, writes):
        for k in reads:
            rd = self.readers.setdefault(k, {})
            if rd.get(tok[0], 0) < tok[1]:
                rd[tok[0]] = tok[1]
        for k in writes:
            self.last_w[k] = tok
            self.readers[k] = {}

    def op(self, eng, fn, reads=(), writes=()):
        waits = self._deps(eng, reads, writes)
        idx = self.cnt[eng]
        self.cnt[eng] += 1
        self.q[eng].append(("op", fn, waits, idx))
        self._record((("eng", eng), idx + 1), reads, writes)

    def dma(self, eng, fn, semkey, reads=(), writes=()):
        waits = self._deps(eng, reads, writes)
        tot = self.dma_tot.get(semkey, 0) + 16
        self.dma_tot[semkey] = tot
        self.q[eng].append(("dma", fn, waits, semkey))
        self._record((("dma", semkey), tot), reads, writes)

    def final_wait(self, eng):
        waits = [(("dma", k), v) for k, v in self.dma_tot.items()]
        self.q[eng].append(("wait", None, waits, None))

    def emit(self, nc, stack):
        sems = {}
        for e in ENGS:
            if e == "sp":
                continue
            for s in range(self.cnt[e] // SEG + 1):
                sems[("eng", e, s)] = stack.enter_context(nc.semaphore(f"p_{e}_{s}"))
        for semkey in self.dma_tot:
            sems[("dma", semkey)] = stack.enter_context(nc.semaphore(f"d_{semkey}"))
        block = stack.enter_context(nc.Block())

        def run(ename):
            def body(engine):
                for kind, fn, waits, aux in self.q[ename]:
                    for k, v in waits:
                        if k[0] == "dma":
                            engine.wait_ge(sems[k], v)
                        else:
                            s = (v - 1) // SEG
                            engine.wait_ge(sems[("eng", k[1], s)], v - s * SEG)
                    if kind == "op":
                        fn(engine).then_inc(sems[("eng", ename, aux // SEG)], 1)
                    elif kind == "dma":
                        fn(engine).then_inc(sems[("dma", aux)], 16)
            return body

        block.tensor(run("pe"))
        block.scalar(run("act"))
        block.vector(run("dve"))
        block.gpsimd(run("pool"))
        block.sync(run("sp"))


class Buf:
    def __init__(self, arena, off, shape, dt):
        n = int(np.prod(shape))
        self.esz = 2 if dt == BF16 else 4
        words = (n * self.esz + 3) // 4
        self.off, self.words, self.shape, self.dt, self.n = off, words, tuple(shape), dt, n
        base = arena[:, off:off + words]
        if dt == BF16:
            base = base.bitcast(BF16)
        self.flat = base
        if len(shape) == 1:
            self.ap = base
        elif len(shape) == 2:
            self.ap = base.rearrange("p (a b) -> p a b", a=shape[0])
        elif len(shape) == 3:
            self.ap = base.rearrange("p (a b c) -> p a b c", a=shape[0], b=shape[1])
        else:
            raise ValueError(shape)

    def K(self, *idx):
        idx = list(idx) + [None] * (len(self.shape) - len(idx))
        box = [(0, s) if i is None else i for i, s in zip(idx, self.shape)]
        strides = [int(np.prod(self.shape[i + 1:])) for i in range(len(self.shape))]
        lead = box[:-1]
        nlead = int(np.prod([hi - lo for lo, hi in lead])) if lead else 1
        ranges = []
        if nlead <= 64:
            def rec(d, base):
                if d == len(self.shape) - 1:
                    ranges.append((base + box[d][0], base + box[d][1]))
                    return
                for i in range(box[d][0], box[d][1]):
                    rec(d + 1, base + i * strides[d])
            rec(0, 0)
        else:
            lo = sum(b[0] * s for b, s in zip(box, strides))
            hi = sum((b[1] - 1) * s for b, s in zip(box, strides)) + 1
            ranges.append((lo, hi))
        keys = set()
        for lo, hi in ranges:
            w0 = self.off + (lo * self.esz) // 4
            w1 = self.off + (hi * self.esz + 3) // 4
            for g in range(w0 // GR, (w1 - 1) // GR + 1):
                keys.add(("g", g))
        return list(keys)


class Arena:
    def __init__(self, nc, stack, words):
        self.t = stack.enter_context(nc.sbuf_tensor("arena", [128, words], F32))
        self.words = words
        self.top = 0

    def alloc(self, shape, dt):
        self.top = (self.top + GR - 1) // GR * GR
        b = Buf(self.t, self.top, shape, dt)
        self.top += b.words
        self.hw = max(getattr(self, 'hw', 0), self.top)
        assert self.top <= self.words, (self.top, self.words)
        return b

    def mark(self):
        return self.top

    def release(self, m):
        self.top = m


def _keys(xs):
    out = []
    for x in xs:
        if isinstance(x, Buf):
            out.extend(x.K())
        elif isinstance(x, list):
            out.extend(x)
        else:
            out.append(x)
    return out


class Prog:
    def __init__(self, nc, stack, debug=()):
        self.nc = nc
        self.st = stack
        self.S = Sched(True)
        self.debug = set(debug)
        self.ps = [stack.enter_context(nc.psum_tensor(f"ps{i}", [128, 512], F32)) for i in range(8)]
        self.psk = [("ps", i) for i in range(8)]
        self.bank_i = 0
        self.nw = 0

    def bank(self):
        i = self.bank_i
        self.bank_i = (i + 1) % 8
        return self.ps[i], self.psk[i]

    def mm(self, out, lhsT, rhs, start=True, stop=True, r=(), w=()):
        self.S.op("pe", lambda e: e.matmul(out, lhsT=lhsT, rhs=rhs, start=start, stop=stop), _keys(r), _keys(w))

    def tr(self, out, in_, ident, r=(), w=()):
        self.S.op("pe", lambda e: e.transpose(out, in_, ident), _keys(r), _keys(w))

    def act(self, out, in_, func, bias=None, scale=None, r=(), w=()):
        kw = {}
        if bias is not None:
            kw["bias"] = bias
        if scale is not None:
            kw["scale"] = scale
        self.S.op("act", lambda e: e.activation(out=out, in_=in_, func=func, **kw), _keys(r), _keys(w))

    def cp(self, eng, out, in_, r=(), w=()):
        if eng == "act":
            self.S.op("act", lambda e: e.copy(out=out, in_=in_), _keys(r), _keys(w))
        else:
            self.S.op(eng, lambda e: e.tensor_copy(out=out, in_=in_), _keys(r), _keys(w))

    def tt(self, eng, out, in0, in1, op, r=(), w=()):
        self.S.op(eng, lambda e: e.tensor_tensor(out=out, in0=in0, in1=in1, op=op), _keys(r), _keys(w))

    def ts(self, eng, out, in0, s1, s2, op0, op1=None, r=(), w=()):
        if op1 is None:
            self.S.op(eng, lambda e: e.tensor_scalar(out=out, in0=in0, scalar1=s1, scalar2=None, op0=op0), _keys(r), _keys(w))
        else:
            self.S.op(eng, lambda e: e.tensor_scalar(out=out, in0=in0, scalar1=s1, scalar2=s2, op0=op0, op1=op1), _keys(r), _keys(w))

    def stt(self, eng, out, in0, scalar, in1, op0, op1, r=(), w=()):
        self.S.op(eng, lambda e: e.scalar_tensor_tensor(out=out, in0=in0, scalar=scalar, in1=in1, op0=op0, op1=op1), _keys(r), _keys(w))

    def recip(self, out, in_, r=(), w=()):
        self.S.op("dve", lambda e: e.reciprocal(out=out, in_=in_), _keys(r), _keys(w))

    def memset(self, eng, ap, val, w=()):
        self.S.op(eng, lambda e: e.memset(ap, val), (), _keys(w))

    def dma(self, eng, out, in_, sem, r=(), w=()):
        self.S.dma(eng, lambda e: e.dma_start(out=out, in_=in_), sem, _keys(r), _keys(w))

    def dump(self, name, buf):
        if name not in self.debug:
            return
        shape = [128] + list(buf.shape)
        d = self.nc.dram_tensor("dbg_" + name, shape, F32 if buf.dt == F32 else BF16, kind="ExternalOutput").ap()
        if len(buf.shape) >= 2 and buf.words > 2048:
            for a in range(buf.shape[0]):
                self.dma("sp", d[:, a], buf.ap[:, a], "dbg", r=[buf])
        else:
            self.dma("sp", d, buf.ap, "dbg", r=[buf])


def v3(ap, a):
    return ap.rearrange("p (a b) -> p a b", a=a)


class _Stop(Exception):
    pass


def build_program(debug=(), stop=None):
    def chk(name):
        if stop == name:
            raise _Stop()

    nc = bass.Bass("TRN2", target_bir_lowering=False)
    din = lambda n, s, dt=F32: nc.dram_tensor(n, list(s), dt, kind="ExternalInput").ap()
    x_in = din("x_in", [T, D])
    cond = din("cond", [D, NSEG])
    sinit = din("sinit", [DEPTH, 2, 4, 128, 64])
    cmlr = din("cmlr", [128, 16])
    pp_d = din("pp", [DEPTH, 128, NPP])
    cf_d = din("cf", [128, NCF])
    cb_d = din("cb", [128, 384], BF16)
    w_ada = din("w_ada", [DEPTH, D, 6 * D])
    w_in = din("w_in", [DEPTH, D, D_IN])
    w_up = din("w_up", [DEPTH, 128, 512])
    a_up = din("a_up", [DEPTH, 128, 512])
    g_up = din("g_up", [DEPTH, 128, 512])
    w_fnet = din("w_fnet", [DEPTH, 4, 64, 64])
    w_pool = din("w_pool", [DEPTH, 4, 64, 64])
    w_out = din("w_out", [DEPTH, D, D])
    w_ff1 = din("w_ff1", [DEPTH, D, 4 * D])
    w_ff2 = din("w_ff2", [DEPTH, 4 * D, D])
    dft_big = din("dft_big", [2, 1024, 1024], BF16)
    dft_sm = din("dft_sm", [2, 256, 256], BF16)
    pm_big = din("pm_big", [4, 1024, 1024], BF16)
    pm_sm = din("pm_sm", [4, 256, 256], BF16)
    icnt_d = din("icnt", [128, 2, T])
    y_out = nc.dram_tensor("y_out", [T, D], F32, kind="ExternalOutput").ap()
    st_out = nc.dram_tensor("st_out", [NSEG, DEPTH, 2, 8, 64, 64], F32, kind="ExternalOutput").ap()

    with ExitStack() as st:
        P = Prog(nc, st, debug)
        A = Arena(nc, st, 53200)
        xT = A.alloc([8, T], F32)
        hT = A.alloc([8, T], BF16)
        modT = A.alloc([48, NSEG], F32)
        A1 = A.alloc([8, NSEG], F32)
        A2 = A.alloc([8, NSEG], F32)
        pp = [A.alloc([NPP], F32) for _ in range(DEPTH)]
        dp = [A.alloc([NDP], F32) for _ in range(DEPTH)]
        cf = A.alloc([NCF], F32)
        cb = A.alloc([384], BF16)
        condT = A.alloc([8, NSEG], F32)
        scond = A.alloc([8, NSEG], BF16)
        cm = A.alloc([16], F32)
        OV = A.mark()

        cfa = cf.ap
        ident_f = cfa[:, CF["ident"]:CF["ident"] + 128]
        blk_f = cfa[:, CF["blk"]:CF["blk"] + 128]
        ones_f = cfa[:, CF["ones"]:CF["ones"] + 128]
        idh = cfa[:, CF["idh"]:CF["idh"] + 64]
        csm = cfa[:, CF["csm"]:CF["csm"] + 256]
        eps_rms = cfa[:, CF["eps"]:CF["eps"] + 1]
        eps_gn = cfa[:, CF["eps"] + 1:CF["eps"] + 2]

        def mask(name):
            return cfa[:, CF[name]:CF[name] + 64].unsqueeze(1).to_broadcast([128, 4, 64])

        idh4 = idh.unsqueeze(1).to_broadcast([128, 4, 64])
        ident_b = cb.ap[:, 0:128]
        dftphi = cb.ap[:, 128:384]

        def ppc(l, name, i=0, n=1):
            return pp[l].ap[:, PP[name] + i:PP[name] + i + n]

        def dpc(l, name, i=0, n=1):
            return dp[l].ap[:, DP[name] + i:DP[name] + i + n]

        P.dma("sp", cf.ap, cf_d, "cst", w=[cf])
        P.dma("sp", cb.ap, cb_d, "cst", w=[cb])
        P.dma("sp", cm.ap, cmlr, "cst", w=[cm])
        for l in range(DEPTH):
            P.dma("sp", pp[l].ap, pp_d[l], "cst", w=[pp[l]])
        P.dma("sp", condT.ap, cond.rearrange("(c p) s -> p c s", p=128), "cst", w=[condT])
        P.act(scond.ap, condT.ap, AF.Silu, r=[condT], w=[scond])
        for l in range(DEPTH):
            P.ts("dve", dpc(l, "omu", 0, 15), ppc(l, "mu", 0, 15), -1.0, 1.0, ALU.mult, ALU.add, r=[pp[l]], w=[dp[l]])
            P.ts("dve", dpc(l, "hmu", 0, 15), ppc(l, "mu", 0, 15), 0.5, None, ALU.mult, r=[pp[l]], w=[dp[l]])
            P.ts("dve", dpc(l, "hw0", 0, 8), ppc(l, "w0", 0, 8), 0.5, None, ALU.mult, r=[pp[l]], w=[dp[l]])
            P.ts("dve", dpc(l, "ha0", 0, 8), ppc(l, "a0", 0, 8), 0.5, None, ALU.mult, r=[pp[l]], w=[dp[l]])
            P.ts("dve", dpc(l, "omka", 0, 4), ppc(l, "ka", 0, 4), -1.0, 1.0, ALU.mult, ALU.add, r=[pp[l]], w=[dp[l]])

        try:
            m0 = A.mark()
            xin = [A.alloc([D], F32) for _ in range(2)]
            for i in range(12):
                xb = xin[i % 2]
                P.dma("sp", xb.ap, x_in[i * 128:(i + 1) * 128, :], f"xin{i % 2}", w=[xb])
                for q in range(2):
                    bk, bkk = P.bank()
                    for cc in range(4):
                        c = q * 4 + cc
                        P.tr(bk[:, cc * 128:(cc + 1) * 128], xb.ap[:, c * 128:(c + 1) * 128], ident_f, r=[xb, cf], w=[bkk])
                    P.cp("act" if q == 0 else "dve", xT.ap[:, q * 4:(q + 1) * 4, i * 128:(i + 1) * 128], v3(bk[:, :], 4),
                         r=[bkk], w=xT.K((q * 4, q * 4 + 4), (i * 128, (i + 1) * 128)))
            A.release(m0)

            chk("x")
            def rmsnorm(l, Acoef, shift0, final=False, yn=None):
                m = A.mark()
                sqb = [A.alloc([512], F32) for _ in range(2)]
                rs = A.alloc([512], F32)
                xnb = [A.alloc([512], F32) for _ in range(2)]
                for tb in range(3):
                    cs = (tb * 512, (tb + 1) * 512)
                    bk, bkk = P.bank()
                    for c in range(8):
                        sq = sqb[c % 2]
                        P.act(sq.ap, xT.ap[:, c, cs[0]:cs[1]], AF.Square, r=xT.K((c, c + 1), cs), w=[sq])
                        P.mm(bk[:, :], ones_f, sq.ap, start=(c == 0), stop=(c == 7), r=[sq, cf], w=[bkk])
                    P.act(rs.ap, bk[:, :], AF.Sqrt, bias=eps_rms, scale=1.0 / D, r=[bkk, cf], w=[rs])
                    P.recip(rs.ap, rs.ap, r=[rs], w=[rs])
                    for c in range(8):
                        if final:
                            P.tt("dve", yn.ap[:, c, :], xT.ap[:, c, cs[0]:cs[1]], rs.ap, ALU.mult, r=xT.K((c, c + 1), cs) + rs.K(), w=yn.K((c, c + 1)))
                            P.act(yn.ap[:, c, :], yn.ap[:, c, :], AF.Identity, scale=ppc(l, "fng", c), r=yn.K((c, c + 1)) + pp[l].K(), w=yn.K((c, c + 1)))
                            continue
                        xn = xnb[c % 2]
                        P.tt("dve", xn.ap, xT.ap[:, c, cs[0]:cs[1]], rs.ap, ALU.mult, r=xT.K((c, c + 1), cs) + rs.K(), w=[xn])
                        for s2 in range(2):
                            sg_ = tb * 2 + s2
                            P.act(hT.ap[:, c, sg_ * 256:(sg_ + 1) * 256], xn.ap[:, s2 * 256:(s2 + 1) * 256], AF.Identity,
                                  bias=modT.ap[:, shift0 + c, sg_:sg_ + 1], scale=Acoef.ap[:, c, sg_:sg_ + 1],
                                  r=xn.K() + modT.K() + Acoef.K(), w=hT.K((c, c + 1), (sg_ * 256, (sg_ + 1) * 256)))
                    if final:
                        yield tb
                A.release(m)

            def run(gen):
                for _ in gen:
                    pass

            def load_w(dst, src_ap, sem):
                P.dma("pool", dst.ap, src_ap.rearrange("(k p) n -> p k n", p=128), sem, w=[dst])

            wcnt = [0]

            def proj_chunk(l, j, wbufs, dst, dst2=None, sc2=None, func=None):
                wb = wbufs[wcnt[0] % 2]
                load_w(wb, w_in[l][:, j * 128:(j + 1) * 128], f"wsm{wcnt[0] % 2}")
                wcnt[0] += 1
                for tb in range(3):
                    cs = (tb * 512, (tb + 1) * 512)
                    bk, bkk = P.bank()
                    for kc in range(8):
                        P.mm(bk[:, :], wb.ap[:, kc, :], hT.ap[:, kc, cs[0]:cs[1]], start=(kc == 0), stop=(kc == 7),
                             r=wb.K() + hT.K((kc, kc + 1), cs), w=[bkk])
                    if func is None:
                        P.cp("act", dst.ap[:, cs[0]:cs[1]], bk[:, :], r=[bkk], w=dst.K(cs))
                    else:
                        P.act(dst.ap[:, cs[0]:cs[1]], bk[:, :], func, r=[bkk], w=dst.K(cs))
                    if dst2 is not None:
                        P.act(dst2.ap[:, cs[0]:cs[1]], bk[:, :], AF.Identity, scale=sc2, r=[bkk], w=dst2.K(cs))

            def shift(l, j, p, p2, nb, out):
                P.tt("dve", nb.ap[:, 1:T - 1], p.ap[:, 0:T - 2], p.ap[:, 2:T], ALU.add, r=[p], w=[nb])
                P.cp("dve", nb.ap[:, 0:1], p.ap[:, 1:2], r=[p], w=[nb])
                P.cp("dve", nb.ap[:, T - 1:T], p.ap[:, T - 2:T - 1], r=[p], w=[nb])
                for b in range(1, NSEG):
                    t = 256 * b
                    P.stt("dve", nb.ap[:, t:t + 1], p.ap[:, t - 1:t], cm.ap[:, b:b + 1], p.ap[:, t + 1:t + 2], ALU.mult, ALU.add, r=[p, cm], w=[nb])
                    P.stt("dve", nb.ap[:, t - 1:t], p.ap[:, t:t + 1], cm.ap[:, 5 + b:6 + b], p.ap[:, t - 2:t - 1], ALU.mult, ALU.add, r=[p, cm], w=[nb])
                P.stt("dve", out.ap, nb.ap, dpc(l, "hmu", j), p2.ap, ALU.mult, ALU.add, r=[nb, p2, dp[l]], w=[out])

            for l in range(DEPTH):
                m_l = A.mark()
                wada = [A.alloc([8, 512], BF16) for _ in range(2)]
                for nb_ in range(12):
                    wb = wada[nb_ % 2]
                    load_w(wb, w_ada[l][:, nb_ * 512:(nb_ + 1) * 512], f"wada{nb_ % 2}")
                    bk, bkk = P.bank()
                    for m in range(4):
                        for kc in range(8):
                            P.mm(bk[:, m * 8:m * 8 + 6], wb.ap[:, kc, m * 128:(m + 1) * 128], scond.ap[:, kc, :],
                                 start=(kc == 0), stop=(kc == 7), r=[wb, scond], w=[bkk])
                    for m in range(4):
                        ch = nb_ * 4 + m
                        P.act(modT.ap[:, ch, :], bk[:, m * 8:m * 8 + 6], AF.Identity, bias=ppc(l, "bada", ch), scale=1.0,
                              r=[bkk, pp[l]], w=modT.K((ch, ch + 1)))
                A.release(m_l)
                for c0_ in (8, 32):
                    P.ts("dve", modT.ap[:, c0_:c0_ + 8, :], modT.ap[:, c0_:c0_ + 8, :], 1.0, None, ALU.add, r=[modT], w=[modT])
                for c in range(8):
                    P.ts("dve", A1.ap[:, c, :], modT.ap[:, 8 + c, :], ppc(l, "n1g", c), None, ALU.mult, r=[modT, pp[l]], w=[A1])
                    P.ts("dve", A2.ap[:, c, :], modT.ap[:, 32 + c, :], ppc(l, "n2g", c), None, ALU.mult, r=[modT, pp[l]], w=[A2])
                P.dump(f"mod{l}", modT)

                chk("mod")
                run(rmsnorm(l, A1, 0))
                P.dump(f"h{l}", hT)

                chk("norm1")
                m_mix = A.mark()
                mixedT = A.alloc([8, T], BF16)
                tw = A.alloc([T], BF16)
                ad = A.alloc([T], BF16)
                sg = A.alloc([T], BF16)
                wsm = [A.alloc([8, 128], BF16) for _ in range(2)]
                wup = A.alloc([512], BF16)
                aup = A.alloc([512], BF16)
                gup = A.alloc([512], BF16)
                P.dma("pool", wup.ap, w_up[l], "wlora", w=[wup])
                P.dma("pool", aup.ap, a_up[l], "wlora", w=[aup])
                P.dma("pool", gup.ap, g_up[l], "wlora", w=[gup])
                m_rw = A.mark()
                R = A.alloc([T], F32)
                Kb = A.alloc([T], F32)
                V = A.alloc([T], F32)
                KK = A.alloc([T], F32)
                BS = A.alloc([T], F32)
                VT = A.alloc([NCH, 64], BF16)
                YS = A.alloc([NCH, 64], F32)
                McT = A.alloc([NCH, 64], BF16)
                Rh = A.alloc([NCH, 64], BF16)
                Gc = A.alloc([NCH, 64], BF16)
                YvT = A.alloc([NCH, 64], BF16)
                STb = A.alloc([64], BF16)
                SI = A.alloc([64], F32)
                STO = A.alloc([NSEG, 64], F32)
                STOt = Buf(A.t, YvT.off, [NSEG, 128], F32)
                MV = A.alloc([2, NCH], F32)
                m_seg = A.mark()
                Pp = A.alloc([T], F32)
                P2 = A.alloc([T], F32)
                NB = A.alloc([T], F32)
                A.release(m_seg)
                class _NS:
                    pass

                def mk_temps(balloc):
                    t_ = _NS()
                    for nm_ in ("SGt", "ASt", "INC", "EXC", "REM", "REMI", "KD", "Bv"):
                        setattr(t_, nm_, A.alloc([256], F32))
                    t_.TMP = t_.SGt
                    t_.E2 = t_.ASt
                    t_.WC = A.alloc([4], F32)
                    for nm_, shp in (("AR", [4, 2, 64]), ("Bt", [4, 64]), ("Kt", [4, 64]), ("BKh", [2, 256]), ("AXb", [4, 2, 64]),
                                     ("BA", [4, 2, 64]), ("KHT", [4, 64]), ("TT", [4, 64]), ("TTt", [4, 64]), ("XB", [4, 64]),
                                     ("NM", [2, 4, 64]), ("AAK", [4, 64]), ("AKR", [4, 64]), ("AU", [4, 2, 64])):
                        setattr(t_, nm_, balloc(shp, BF16))
                    return t_

                TA = mk_temps(A.alloc)
                _bo = [mixedT.off + 4 * T // 2]

                def _balloc_b(shp, dt):
                    bb_ = Buf(A.t, _bo[0], shp, dt)
                    _bo[0] += (bb_.words + GR - 1) // GR * GR
                    assert _bo[0] <= mixedT.off + mixedT.words
                    return bb_

                TB = mk_temps(_balloc_b)
                A.top = max(A.top, NB.off + T)
                m_rw_end = A.mark()

                def w3(b):
                    return b.flat.rearrange("p (c x) -> p c x", c=4)

                for j, dstb, fn, sc in ((12, tw, AF.Tanh, 1.0), (13, ad, AF.Identity, 1.0), (14, sg, AF.Tanh, 0.5)):
                    proj_chunk(l, j, wsm, Pp, P2, dpc(l, "omu", j))
                    shift(l, j, Pp, P2, NB, P2)
                    if j == 14:
                        P.act(P2.ap, P2.ap, AF.Tanh, scale=0.5, r=[P2], w=[P2])
                        P.ts("dve", sg.ap, P2.ap, 0.5, 0.5, ALU.mult, ALU.add, r=[P2], w=[sg])
                    else:
                        P.act(dstb.ap, P2.ap, fn, r=[P2], w=[dstb])

                chk("lora")
                for p in range(4):
                    h_ = [slice(0, 64), slice(64, 128)]
                    for j, dstb in ((p, R), (4 + p, Kb), (8 + p, V)):
                        proj_chunk(l, j, wsm, Pp, P2, dpc(l, "omu", j))
                        shift(l, j, Pp, P2, NB, dstb)
                    if p == 0:
                        P.dump(f"r{l}", R); P.dump(f"k{l}", Kb); P.dump(f"v{l}", V)
                    chk("rkv")
                    P.ts("dve", KK.ap, Kb.ap, ppc(l, "kk", p), None, ALU.mult, r=[Kb, pp[l]], w=[KK])
                    for tb in range(3):
                        cs = (tb * 512, (tb + 1) * 512)
                        P.act(Pp.ap[:, cs[0]:cs[1]], KK.ap[:, cs[0]:cs[1]], AF.Square, r=KK.K(cs), w=Pp.K(cs))
                        bk, bkk = P.bank()
                        P.mm(bk[:, :], blk_f, Pp.ap[:, cs[0]:cs[1]], r=Pp.K(cs) + cf.K(), w=[bkk])
                        P.act(P2.ap[:, cs[0]:cs[1]], bk[:, :], AF.Sqrt, r=[bkk], w=P2.K(cs))
                    P.ts("dve", P2.ap, P2.ap, 1e-12, None, ALU.max, r=[P2], w=[P2])
                    P.recip(P2.ap, P2.ap, r=[P2], w=[P2])
                    P.tt("dve", KK.ap, KK.ap, P2.ap, ALU.mult, r=[KK, P2], w=[KK])
                    chk("kk")
                    for sgm in range(NSEG):
                        bk, bkk = P.bank()
                        for c4 in range(4):
                            ch = sgm * 4 + c4
                            for e in range(2):
                                P.mm(bk[h_[e], c4 * 64:(c4 + 1) * 64], V.ap[h_[e], ch * 64:(ch + 1) * 64], ident_f[h_[e], h_[e]],
                                     r=V.K((ch * 64, (ch + 1) * 64)) + cf.K(), w=[bkk])
                        P.cp("act", VT.ap[:, sgm * 4:(sgm + 1) * 4, :], v3(bk[:, 0:256], 4), r=[bkk], w=VT.K((sgm * 4, (sgm + 1) * 4)))

                    chk("vt")
                    for d in range(2):
                        dh = slice(d * 64, (d + 1) * 64)
                        mS = mask("mSU" if d == 0 else "mSL")
                        mI = mask("mIU" if d == 0 else "mIL")
                        mST = mask("mSL" if d == 0 else "mSU")
                        def seg_body(Tm, bankf, sgm):
                            SGt, ASt, INC, EXC, REM, REMI, KD, Bv, TMP, E2, WC, AR, Bt, Kt, BKh, AXb, BA, KHT, TT, TTt, XB, NM, AAK, AKR, AU = (Tm.SGt, Tm.ASt, Tm.INC, Tm.EXC, Tm.REM, Tm.REMI, Tm.KD, Tm.Bv, Tm.TMP, Tm.E2, Tm.WC, Tm.AR, Tm.Bt, Tm.Kt, Tm.BKh, Tm.AXb, Tm.BA, Tm.KHT, Tm.TT, Tm.TTt, Tm.XB, Tm.NM, Tm.AAK, Tm.AKR, Tm.AU)
                            cs = (sgm * 256, (sgm + 1) * 256)
                            csl = slice(cs[0], cs[1])
                            bk, bkk = bankf()
                            P.mm(bk[:, 0:256], wup.ap[dh, p * 128:(p + 1) * 128], tw.ap[dh, csl], r=[wup] + tw.K(cs), w=[bkk])
                            P.mm(bk[:, 256:512], aup.ap[dh, p * 128:(p + 1) * 128], ad.ap[dh, csl], r=[aup] + ad.K(cs), w=[bkk])
                            P.act(SGt.ap, bk[:, 0:256], AF.Tanh, bias=dpc(l, "hw0", d * 4 + p), scale=0.5, r=[bkk, dp[l]], w=[SGt])
                            P.act(ASt.ap, bk[:, 256:512], AF.Tanh, bias=dpc(l, "ha0", d * 4 + p), scale=0.5, r=[bkk, dp[l]], w=[ASt])
                            P.ts("dve", SGt.ap, SGt.ap, 0.5 * C0, 0.5 * C0, ALU.mult, ALU.add, r=[SGt], w=[SGt])
                            P.ts("dve", ASt.ap, ASt.ap, 0.5, 0.5, ALU.mult, ALU.add, r=[ASt], w=[ASt])
                            yield
                            P.S.op("dve", lambda e, o=INC.ap, d0=csm, d1=SGt.ap: e.tensor_tensor_scan(out=o, data0=d0, data1=d1, initial=0.0, op0=ALU.mult, op1=ALU.add),
                                   _keys([SGt, cf]), _keys([INC]))
                            inc3 = v3(INC.ap, 4)
                            P.tt("dve", EXC.ap, INC.ap, SGt.ap, ALU.subtract, r=[INC, SGt], w=[EXC])
                            P.tt("dve", v3(REM.ap, 4), inc3[:, :, 63:64].to_broadcast([128, 4, 64]), inc3, ALU.subtract, r=[INC], w=[REM])
                            P.tt("dve", REMI.ap, REM.ap, SGt.ap, ALU.add, r=[REM, SGt], w=[REMI])
                            P.act(WC.ap, inc3[:, :, 63], AF.Exp, scale=-1.0, r=[INC], w=[WC])
                            cI, cE, cR = (INC, EXC, REM) if d == 0 else (REMI, REM, EXC)
                            yield
                            P.ts("dve", TMP.ap, ASt.ap, ppc(l, "ka", p), dpc(l, "omka", p), ALU.mult, ALU.add, r=[ASt, pp[l], dp[l]], w=[TMP])
                            P.tt("dve", KD.ap, TMP.ap, Kb.ap[:, csl], ALU.mult, r=TMP.K() + Kb.K(cs), w=[KD])
                            P.tt("dve", Bv.ap, KK.ap[:, csl], ASt.ap, ALU.mult, r=KK.K(cs) + ASt.K(), w=[Bv])
                            if d == 0:
                                P.stt("dve", BS.ap[:, csl], R.ap[:, csl], ppc(l, "rk", p), KD.ap, ALU.mult, ALU.mult, r=R.K(cs) + KD.K() + pp[l].K(), w=BS.K(cs))
                            else:
                                P.stt("dve", TMP.ap, R.ap[:, csl], ppc(l, "rk", p), KD.ap, ALU.mult, ALU.mult, r=R.K(cs) + KD.K() + pp[l].K(), w=[TMP])
                                P.tt("dve", BS.ap[:, csl], BS.ap[:, csl], TMP.ap, ALU.add, r=BS.K(cs) + TMP.K(), w=BS.K(cs))
                            yield
                            P.act(E2.ap, cI.ap, AF.Exp, scale=-1.0, r=[cI], w=[E2])
                            P.tt("dve", AR.ap[:, :, 1, :], v3(R.ap[:, csl], 4), v3(E2.ap, 4), ALU.mult, r=R.K(cs) + E2.K(), w=[AR])
                            P.act(cI.ap, cI.ap, AF.Exp, scale=1.0, r=[cI], w=[cI])
                            P.tt("dve", Bt.ap, v3(Bv.ap, 4), v3(cI.ap, 4), ALU.mult, r=[Bv, cI], w=[Bt])
                            P.tt("dve", Kt.ap, v3(KD.ap, 4), v3(cI.ap, 4), ALU.mult, r=[KD, cI], w=[Kt])
                            P.act(cE.ap, cE.ap, AF.Exp, scale=-1.0, r=[cE], w=[cE])
                            P.stt("dve", AR.ap[:, :, 0, :], v3(KK.ap[:, csl], 4), -1.0, v3(cE.ap, 4), ALU.mult, ALU.mult, r=KK.K(cs) + cE.K(), w=[AR])
                            P.act(cR.ap, cR.ap, AF.Exp, scale=-1.0, r=[cR], w=[cR])
                            P.tt("dve", BKh.ap[:, 0, :], Bv.ap, cR.ap, ALU.mult, r=[Bv, cR], w=[BKh])
                            P.tt("dve", BKh.ap[:, 1, :], KD.ap, cR.ap, ALU.mult, r=[KD, cR], w=[BKh])
                            yield
                            bk, bkk = bankf()
                            bb = bk[:, :].bitcast(BF16)
                            for c4 in range(4):
                                for e in range(2):
                                    idb = ident_b[h_[e], h_[e]]
                                    P.tr(bb[h_[e], c4 * 64:(c4 + 1) * 64], AR.ap[h_[e], c4, 0, :], idb, r=[AR, cb], w=[bkk])
                                    P.tr(bb[h_[e], 256 + c4 * 64:256 + (c4 + 1) * 64], BKh.ap[h_[e], 0, c4 * 64:(c4 + 1) * 64], idb, r=[BKh, cb], w=[bkk])
                                    P.tr(bb[h_[e], 512 + c4 * 64:512 + (c4 + 1) * 64], BKh.ap[h_[e], 1, c4 * 64:(c4 + 1) * 64], idb, r=[BKh, cb], w=[bkk])
                            P.cp("act", AXb.ap[:, :, 0, :], v3(bb[:, 0:256], 4), r=[bkk], w=[AXb])
                            P.cp("act", BA.ap[:, :, 0, :], v3(bb[:, 256:512], 4), r=[bkk], w=[BA])
                            P.cp("act", KHT.ap, v3(bb[:, 512:768], 4), r=[bkk], w=[KHT])
                            yield
                            bA, kA = bankf()
                            bB, kB = bankf()
                            bC, kC = bankf()
                            bA3, bB3 = v3(bA[:, :], 4), v3(bB[:, :], 4)
                            for c4 in range(4):
                                for e in range(2):
                                    h = h_[e]
                                    P.mm(bA[h, c4 * 128:(c4 + 1) * 128], Bt.ap[h, c4, :], w3(AR)[h, c4, :], r=[Bt, AR], w=[kA])
                                    P.mm(bB[h, c4 * 128:(c4 + 1) * 128], Kt.ap[h, c4, :], w3(AR)[h, c4, :], r=[Kt, AR], w=[kB])
                                    P.mm(bC[h, c4 * 64:(c4 + 1) * 64], AR.ap[h, c4, 0, :], Bt.ap[h, c4, :], r=[Bt, AR], w=[kC])
                            yield
                            bC3 = v3(bC[:, 0:256], 4)
                            P.tt("dve", BA.ap[:, :, 1, :], bA3[:, :, 64:128], mI, ALU.mult, r=[kA, cf], w=[BA])
                            P.tt("dve", AAK.ap, bB3[:, :, 0:64], mS, ALU.mult, r=[kB, cf], w=[AAK])
                            P.tt("dve", AKR.ap, bB3[:, :, 64:128], mI, ALU.mult, r=[kB, cf], w=[AKR])
                            yield
                            up, lo = ("mU", "mL") if d == 0 else ("mL", "mU")
                            P.tt("dve", TT.ap, bA3[:, :, 0:64], mask(up + "0"), ALU.mult, r=[kA, cf], w=[TT])
                            P.tt("dve", TT.ap, TT.ap, idh4, ALU.add, r=[TT, cf], w=[TT])
                            P.tt("dve", TTt.ap, bC3, mask(lo + "0"), ALU.mult, r=[kC, cf], w=[TTt])
                            P.tt("dve", TTt.ap, TTt.ap, idh4, ALU.add, r=[TTt, cf], w=[TTt])
                            yield
                            bX, kX = bankf()
                            bY, kY_ = bankf()
                            bZ, kZ = bankf()
                            for jr in range(1, 6):
                                P.tt("dve", NM.ap[:, jr % 2, :, :], bC3, mask(lo + str(jr)), ALU.mult, r=[kC, cf], w=NM.K((jr % 2, jr % 2 + 1)))
                                for c4 in range(4):
                                    for e in range(2):
                                        h = h_[e]
                                        P.mm(bX[h, c4 * 64:(c4 + 1) * 64], NM.ap[h, jr % 2, c4, :], TT.ap[h, c4, :], r=NM.K((jr % 2, jr % 2 + 1)) + TT.K(), w=[kX])
                                yield
                                P.cp("act", XB.ap, v3(bX[:, 0:256], 4), r=[kX], w=[XB])
                                yield
                                for c4 in range(4):
                                    for e in range(2):
                                        h = h_[e]
                                        P.mm(bY[h, c4 * 64:(c4 + 1) * 64], TTt.ap[h, c4, :], XB.ap[h, c4, :], r=[TTt, XB], w=[kY_])
                                        if jr < 5:
                                            P.mm(bZ[h, c4 * 64:(c4 + 1) * 64], XB.ap[h, c4, :], TTt.ap[h, c4, :], r=[TTt, XB], w=[kZ])
                                yield
                                P.tt("dve", TT.ap, TT.ap, v3(bY[:, 0:256], 4), ALU.add, r=[kY_, TT], w=[TT])
                                if jr < 5:
                                    P.tt("dve", TTt.ap, TTt.ap, v3(bZ[:, 0:256], 4), ALU.add, r=[kZ, TTt], w=[TTt])
                            yield
                            for c4 in range(4):
                                ch = sgm * 4 + c4
                                for e in range(2):
                                    h = h_[e]
                                    P.mm(bB[h, c4 * 64:(c4 + 1) * 64], AAK.ap[h, c4, :], VT.ap[h, ch, :], r=AAK.K() + VT.K((ch, ch + 1)), w=[kB])
                            P.cp("act", AXb.ap[:, :, 1, :], v3(bB[:, 0:256], 4), r=[kB], w=[AXb])
                            yield
                            for c4 in range(4):
                                for e in range(2):
                                    h = h_[e]
                                    P.mm(bA[h, c4 * 128:(c4 + 1) * 128], TT.ap[h, c4, :], w3(AXb)[h, c4, :], r=[TT, AXb], w=[kA])
                            P.cp("act", w3(AU), bA3, r=[kA], w=[AU])
                            yield
                            bM, kM = bankf()
                            bM3 = v3(bM[:, :], 4)
                            for c4 in range(4):
                                ch = sgm * 4 + c4
                                for e in range(2):
                                    h = h_[e]
                                    P.mm(bM[h, c4 * 128:(c4 + 1) * 128], AU.ap[h, c4, 0, :], w3(BA)[h, c4, :], r=[AU, BA], w=[kM])
                                    P.mm(bB[h, c4 * 64:(c4 + 1) * 64], BA.ap[h, c4, 0, :], AU.ap[h, c4, 1, :], start=True, stop=False, r=[AU, BA], w=[kB])
                                    P.mm(bB[h, c4 * 64:(c4 + 1) * 64], KHT.ap[h, c4, :], VT.ap[h, ch, :], start=False, stop=True, r=KHT.K() + VT.K((ch, ch + 1)), w=[kB])
                                    P.mm(bC[h, c4 * 64:(c4 + 1) * 64], BA.ap[h, c4, 1, :], AU.ap[h, c4, 1, :], start=True, stop=False, r=[AU, BA], w=[kC])
                                    P.mm(bC[h, c4 * 64:(c4 + 1) * 64], AKR.ap[h, c4, :], VT.ap[h, ch, :], start=False, stop=True, r=AKR.K() + VT.K((ch, ch + 1)), w=[kC])
                            yield
                            chs = (sgm * 4, (sgm + 1) * 4)
                            for c4 in range(4):
                                ch = sgm * 4 + c4
                                P.stt("dve", McT.ap[:, ch, :], idh, WC.ap[:, c4:c4 + 1], bM3[:, c4, 0:64], ALU.mult, ALU.add, r=[kM, cf, WC], w=McT.K((ch, ch + 1)))
                            P.tt("dve", Rh.ap[:, chs[0]:chs[1], :], bM3[:, :, 64:128], AR.ap[:, :, 1, :], ALU.add, r=[kM, AR], w=Rh.K(chs))
                            P.cp("act", Gc.ap[:, chs[0]:chs[1], :], v3(bB[:, 0:256], 4), r=[kB], w=Gc.K(chs))
                            P.cp("act", YvT.ap[:, chs[0]:chs[1], :], v3(bC[:, 0:256], 4), r=[kC], w=YvT.K(chs))


                        def _chain(Tm, bankf, segs):
                            for sg__ in segs:
                                yield from seg_body(Tm, bankf, sg__)

                        _bi = {0: [0], 1: [0]}

                        def _bankf(which):
                            def f():
                                i_ = which * 4 + (3, 3, 0, 1, 2, 3, 0, 1, 3)[_bi[which][0] % 9]
                                _bi[which][0] += 1
                                return P.ps[i_], P.psk[i_]
                            return f

                        ga = _chain(TA, _bankf(0), [0, 2, 4])
                        gb = _chain(TB, _bankf(1), [1, 3, 5])
                        alive = [ga]
                        lead = 1
                        while alive:
                            for g_ in list(alive):
                                try:
                                    next(g_)
                                except StopIteration:
                                    alive.remove(g_)
                            if lead is not None:
                                lead -= 1
                                if lead == 0:
                                    alive.append(gb)
                                    lead = None
                        if lead is not None:
                            for _ in gb:
                                pass
                        P.dma("sp", SI.ap, sinit[l, d, p], "sinit", w=[SI])
                        if d == 1:
                            P.tt("dve", YvT.ap, YvT.ap, YS.ap, ALU.add, r=[YvT, YS], w=[YvT])
                        order = list(range(NCH)) if d == 0 else list(range(NCH - 1, -1, -1))
                        for idx, ch in enumerate(order):
                            sgm = ch // 4
                            first = (ch % 4 == 0) if d == 0 else (ch % 4 == 3)
                            last = (ch % 4 == 3) if d == 0 else (ch % 4 == 0)
                            if first:
                                if d == 0:
                                    rule = "init" if sgm == 0 else ("carry" if sgm in (1, 2, 3) else "zero")
                                else:
                                    rule = "init" if sgm == 3 else ("carry" if sgm in (0, 1, 2) else "zero")
                                if rule == "init":
                                    P.cp("dve", STb.ap, SI.ap, r=[SI], w=[STb])
                                elif rule == "zero":
                                    P.memset("dve", STb.ap, 0.0, w=[STb])
                                else:
                                    P.ts("dve", STb.ap, STb.ap, cm.ap[:, 0:1], None, ALU.mult, r=[STb, cm], w=[STb])
                            bY, kY = P.bank()
                            bS, kS = P.bank()
                            for e in range(2):
                                h = h_[e]
                                P.mm(bS[h, 0:64], McT.ap[h, ch, :], STb.ap[h, :], r=McT.K((ch, ch + 1)) + STb.K(), w=[kS])
                            for e in range(2):
                                h = h_[e]
                                P.mm(bY[h, 0:64], Rh.ap[h, ch, :], STb.ap[h, :], r=Rh.K((ch, ch + 1)) + STb.K(), w=[kY])
                            P.tt("dve", STb.ap, bS[:, 0:64], Gc.ap[:, ch, :], ALU.add, r=[kS] + Gc.K((ch, ch + 1)), w=[STb])
                            if last:
                                P.tt("dve", STO.ap[:, sgm, :], bS[:, 0:64], Gc.ap[:, ch, :], ALU.add, r=[kS] + Gc.K((ch, ch + 1)), w=STO.K((sgm, sgm + 1)))
                            P.tt("dve", YS.ap[:, ch, :], bY[:, 0:64], YvT.ap[:, ch, :], ALU.add, r=[kY] + YvT.K((ch, ch + 1)), w=YS.K((ch, ch + 1)))
                        chk("pass")
                        for half in range(2):
                            bk, bkk = P.bank()
                            for s3 in range(3):
                                sgm = half * 3 + s3
                                P.tr(bk[0:64, s3 * 128:(s3 + 1) * 128], STO.ap[:, sgm, :], ident_f, r=STO.K((sgm, sgm + 1)) + cf.K(), w=[bkk])
                            P.cp("act", STOt.ap[0:64, half * 3:(half + 1) * 3, :], v3(bk[0:64, 0:384], 3), r=[bkk], w=STOt.K((half * 3, (half + 1) * 3)))
                        for sgm in range(NSEG):
                            P.dma("sp", st_out[sgm, l, d, 2 * p:2 * p + 2, :, :].rearrange("e v k -> v e k"),
                                  STOt.ap[0:64, sgm, :].rearrange("v (e k) -> v e k", e=2), "stout", r=STOt.K((sgm, sgm + 1)))

                    chk("stout")
                    if p == 0:
                        P.dump(f"ys{l}", YS)
                    YN = Buf(A.t, P2.off, [NCH, 64], F32)
                    SQb = Buf(A.t, NB.off, [NCH, 64], F32)
                    MEAN = _NS()
                    MEAN.ap = MV.ap[:, 0, :]
                    VAR = _NS()
                    VAR.ap = MV.ap[:, 1, :]
                    P.S.op("dve", lambda e, o=MEAN.ap, i=YS.ap: e.tensor_reduce(out=o, in_=i, axis=AX.X, op=ALU.add), _keys([YS]), _keys([MV]))
                    P.ts("dve", MEAN.ap, MEAN.ap, 1.0 / 64, None, ALU.mult, r=[MV], w=[MV])
                    P.tt("dve", YS.ap, YS.ap, MEAN.ap.unsqueeze(2).to_broadcast([128, NCH, 64]), ALU.subtract, r=[YS, MV], w=[YS])
                    P.act(SQb.ap, YS.ap, AF.Square, r=[YS], w=[SQb])
                    P.S.op("dve", lambda e, o=VAR.ap, i=SQb.ap: e.tensor_reduce(out=o, in_=i, axis=AX.X, op=ALU.add), _keys([SQb]), _keys([MV]))
                    P.act(VAR.ap, VAR.ap, AF.Sqrt, bias=eps_gn, scale=1.0 / 64, r=[MV, cf], w=[MV])
                    P.recip(VAR.ap, VAR.ap, r=[MV], w=[MV])
                    P.tt("dve", YS.ap, YS.ap, VAR.ap.unsqueeze(2).to_broadcast([128, NCH, 64]), ALU.mult, r=[YS, MV], w=[YS])
                    for g3 in range(3):
                        bk, bkk = P.bank()
                        for cc in range(8):
                            ch = g3 * 8 + cc
                            for e in range(2):
                                P.mm(bk[h_[e], cc * 64:(cc + 1) * 64], YS.ap[h_[e], ch, :], ident_f[h_[e], h_[e]], r=YS.K((ch, ch + 1)) + cf.K(), w=[bkk])
                        P.act(YN.flat[:, g3 * 512:(g3 + 1) * 512], bk[:, :], AF.Identity, bias=ppc(l, "lnb", p), scale=ppc(l, "lnw", p),
                              r=[bkk, pp[l]], w=YN.K((g3 * 8, (g3 + 1) * 8)))
                    for tb in range(3):
                        cs = (tb * 512, (tb + 1) * 512)
                        csl = slice(cs[0], cs[1])
                        bk, bkk = P.bank()
                        P.mm(bk[:, :], blk_f, BS.ap[:, csl], r=BS.K(cs) + cf.K(), w=[bkk])
                        P.tt("dve", Pp.ap[:, csl], bk[:, :], V.ap[:, csl], ALU.mult, r=[bkk] + V.K(cs), w=Pp.K(cs))
                        P.tt("dve", YN.flat[:, csl], YN.flat[:, csl], Pp.ap[:, csl], ALU.add, r=YN.K((tb * 8, (tb + 1) * 8)) + Pp.K(cs), w=YN.K((tb * 8, (tb + 1) * 8)))
                        bk2, bkk2 = P.bank()
                        P.mm(bk2[:, :], gup.ap[:, p * 128:(p + 1) * 128], sg.ap[:, csl], r=[gup] + sg.K(cs), w=[bkk2])
                        P.tt("dve", mixedT.ap[:, p, csl], YN.flat[:, csl], bk2[:, :], ALU.mult, r=YN.K((tb * 8, (tb + 1) * 8)) + [bkk2], w=mixedT.K((p, p + 1), cs))
                P.dump(f"mixa{l}", mixedT)
                A.release(m_rw)

                chk("rwkv")
                m_f = A.mark()
                fT = A.alloc([2, T], BF16)
                FC = A.alloc([12, 4, 128], BF16)
                DCb = A.alloc([8, 1024], BF16)
                DSb = A.alloc([8, 1024], BF16)
                DCs = A.alloc([2, 256], BF16)
                DSs = A.alloc([2, 256], BF16)
                zT = A.alloc([2, T], BF16)
                WFb = [A.alloc([128], BF16) for _ in range(2)]
                Pf = A.alloc([T], F32)
                P.dma("sp", DCb.ap, dft_big[0].rearrange("(k p) n -> p k n", p=128), "dft", w=[DCb])
                P.dma("sp", DSb.ap, dft_big[1].rearrange("(k p) n -> p k n", p=128), "dft", w=[DSb])
                P.dma("sp", DCs.ap, dft_sm[0].rearrange("(k p) n -> p k n", p=128), "dft", w=[DCs])
                P.dma("sp", DSs.ap, dft_sm[1].rearrange("(k p) n -> p k n", p=128), "dft", w=[DSs])
                for j in range(2):
                    P.memset("dve", WFb[j].ap, 0.0, w=[WFb[j]])
                    for e in range(2):
                        P.dma("pool", WFb[j].ap[e * 64:(e + 1) * 64, e * 64:(e + 1) * 64], w_fnet[l, 2 * j + e], "wfn", w=[WFb[j]])
                    proj_chunk(l, 15 + j, wsm, Pf)
                    P.cp("dve", fT.ap[:, j, :], Pf.ap, r=[Pf], w=fT.K((j, j + 1)))
                for i in range(12):
                    bk, bkk = P.bank()
                    for j in range(2):
                        P.mm(bk[:, j * 256:(j + 1) * 256], fT.ap[:, j, i * 128:(i + 1) * 128], dftphi, r=fT.K((j, j + 1), (i * 128, (i + 1) * 128)) + cb.K(), w=[bkk])
                    P.cp("act" if i % 2 else "dve", FC.ap[:, i, :, :], v3(bk[:, :], 4), r=[bkk], w=FC.K((i, i + 1)))
                for j in range(2):
                    for hb in range(2):
                        bk, bkk = P.bank()
                        n = 0
                        for lt in range(8):
                            for cs_, Dm in ((0, DCb), (1, DSb)):
                                P.mm(bk[:, :], FC.ap[:, lt, 2 * j + cs_, :], Dm.ap[:, lt, hb * 512:(hb + 1) * 512], start=(n == 0), stop=(n == 15),
                                     r=FC.K((lt, lt + 1)) + Dm.K((lt, lt + 1)), w=[bkk])
                                n += 1
                        P.cp("act", zT.ap[:, j, hb * 512:(hb + 1) * 512], bk[:, :], r=[bkk], w=zT.K((j, j + 1), (hb * 512, (hb + 1) * 512)))
                    for s45 in range(2):
                        bk, bkk = P.bank()
                        n = 0
                        for lt2 in range(2):
                            lt = 8 + 2 * s45 + lt2
                            for cs_, Dm in ((0, DCs), (1, DSs)):
                                P.mm(bk[:, 0:256], FC.ap[:, lt, 2 * j + cs_, :], Dm.ap[:, lt2, :], start=(n == 0), stop=(n == 3),
                                     r=FC.K((lt, lt + 1)) + Dm.K(), w=[bkk])
                                n += 1
                        t0 = 1024 + 256 * s45
                        P.cp("dve", zT.ap[:, j, t0:t0 + 256], bk[:, 0:256], r=[bkk], w=zT.K((j, j + 1), (t0, t0 + 256)))
                    for tb in range(3):
                        cs = (tb * 512, (tb + 1) * 512)
                        bk, bkk = P.bank()
                        P.mm(bk[:, :], WFb[j].ap, zT.ap[:, j, cs[0]:cs[1]], r=WFb[j].K() + zT.K((j, j + 1), cs), w=[bkk])
                        P.cp("act", mixedT.ap[:, 4 + j, cs[0]:cs[1]], bk[:, :], r=[bkk], w=mixedT.K((4 + j, 5 + j), cs))
                A.release(m_f)

                chk("fnet")
                m_p = A.mark()
                PTM = A.alloc([12, 256], BF16)
                pT = A.alloc([2, T], F32)
                PMb = [A.alloc([8, 1024], BF16)] * 2
                PMs = A.alloc([4, 2, 256], BF16)
                ICN = A.alloc([2, T], F32)
                dT = A.alloc([2, T], BF16)
                WPc = A.alloc([8, 256], BF16)
                WPb = [A.alloc([128], BF16) for _ in range(2)]
                PD = A.alloc([T], F32)
                load_w(WPc, w_in[l][:, 2176:2432], "wpc")
                P.dma("sp", ICN.ap, icnt_d, "icn", w=[ICN])
                for g in range(4):
                    P.dma("sp", PMs.ap[:, g, :, :], pm_sm[g].rearrange("(k p) n -> p k n", p=128), "pms", w=[PMs])
                for j in range(2):
                    P.memset("dve", WPb[j].ap, 0.0, w=[WPb[j]])
                    for e in range(2):
                        P.dma("pool", WPb[j].ap[e * 64:(e + 1) * 64, e * 64:(e + 1) * 64], w_pool[l, 2 * j + e], "wpl", w=[WPb[j]])
                    proj_chunk(l, 17 + j, wsm, PD)
                    P.cp("dve", pT.ap[:, j, :], PD.ap, r=[PD], w=pT.K((j, j + 1)))
                for i in range(12):
                    bk, bkk = P.bank()
                    for kc in range(8):
                        P.mm(bk[:, 0:256], hT.ap[:, kc, i * 128:(i + 1) * 128], WPc.ap[:, kc, :], start=(kc == 0), stop=(kc == 7),
                             r=hT.K((kc, kc + 1), (i * 128, (i + 1) * 128)) + WPc.K(), w=[bkk])
                    P.cp("act" if i % 2 else "dve", PTM.ap[:, i, :], bk[:, 0:256], r=[bkk], w=PTM.K((i, i + 1)))
                for g in range(4):
                    j, e = g // 2, g % 2
                    h = slice(e * 64, (e + 1) * 64)
                    pmb = PMb[g % 2]
                    P.dma("sp", pmb.ap, pm_big[g].rearrange("(k p) n -> p k n", p=128), "pmb0", w=[pmb])
                    for hb in range(2):
                        bk, bkk = P.bank()
                        for lt in range(8):
                            P.mm(bk[h, :], PTM.ap[:, lt, g * 64:(g + 1) * 64], pmb.ap[:, lt, hb * 512:(hb + 1) * 512], start=(lt == 0), stop=(lt == 7),
                                 r=PTM.K((lt, lt + 1)) + pmb.K((lt, lt + 1)), w=[bkk])
                        cs = (hb * 512, (hb + 1) * 512)
                        P.tt("dve", PD.ap[h, cs[0]:cs[1]], bk[h, :], ICN.ap[h, j, cs[0]:cs[1]], ALU.mult, r=[bkk] + ICN.K((j, j + 1), cs), w=PD.K(cs))
                    bk, bkk = P.bank()
                    for s45 in range(2):
                        for lt2 in range(2):
                            lt = 8 + 2 * s45 + lt2
                            P.mm(bk[h, s45 * 256:(s45 + 1) * 256], PTM.ap[:, lt, g * 64:(g + 1) * 64], PMs.ap[:, g, lt2, :], start=(lt2 == 0), stop=(lt2 == 1),
                                 r=PTM.K((lt, lt + 1)) + PMs.K((g, g + 1)), w=[bkk])
                    P.tt("dve", PD.ap[h, 1024:1536], bk[h, :], ICN.ap[h, j, 1024:1536], ALU.mult, r=[bkk] + ICN.K((j, j + 1), (1024, 1536)), w=PD.K((1024, 1536)))
                    if e == 1:
                        P.tt("dve", dT.ap[:, j, :], PD.ap, pT.ap[:, j, :], ALU.subtract, r=PD.K() + pT.K((j, j + 1)), w=dT.K((j, j + 1)))
                        for tb in range(3):
                            cs = (tb * 512, (tb + 1) * 512)
                            bk2, bkk2 = P.bank()
                            P.mm(bk2[:, :], WPb[j].ap, dT.ap[:, j, cs[0]:cs[1]], r=WPb[j].K() + dT.K((j, j + 1), cs), w=[bkk2])
                            P.act(mixedT.ap[:, 6 + j, cs[0]:cs[1]], bk2[:, :], AF.Identity, scale=ppc(l, "psc", j), r=[bkk2, pp[l]], w=mixedT.K((6 + j, 7 + j), cs))
                A.release(m_p)
                P.dump(f"mix{l}", mixedT)

                chk("pool")
                for mo in range(8):
                    wb = wsm[wcnt[0] % 2]
                    load_w(wb, w_out[l][:, mo * 128:(mo + 1) * 128], f"wsm{wcnt[0] % 2}")
                    wcnt[0] += 1
                    for tb in range(3):
                        cs = (tb * 512, (tb + 1) * 512)
                        bk, bkk = P.bank()
                        for kc in range(8):
                            P.mm(bk[:, :], wb.ap[:, kc, :], mixedT.ap[:, kc, cs[0]:cs[1]], start=(kc == 0), stop=(kc == 7),
                                 r=wb.K() + mixedT.K((kc, kc + 1), cs), w=[bkk])
                        for s2 in range(2):
                            sg_ = tb * 2 + s2
                            c2 = (sg_ * 256, (sg_ + 1) * 256)
                            P.stt("dve", xT.ap[:, mo, c2[0]:c2[1]], bk[:, s2 * 256:(s2 + 1) * 256], modT.ap[:, 16 + mo, sg_:sg_ + 1], xT.ap[:, mo, c2[0]:c2[1]],
                                  ALU.mult, ALU.add, r=[bkk] + modT.K() + xT.K((mo, mo + 1), c2), w=xT.K((mo, mo + 1), c2))
                A.release(m_mix)
                P.dump(f"xmid{l}", xT)

                chk("wout")
                run(rmsnorm(l, A2, 24))
                m_ff = A.mark()
                hid = A.alloc([32, T], BF16)
                w1b = [A.alloc([8, 128], BF16) for _ in range(2)]
                w2b = [A.alloc([32, 128], BF16) for _ in range(2)]
                rl = [A.alloc([512], F32) for _ in range(2)]
                for fo in range(32):
                    wb = w1b[fo % 2]
                    load_w(wb, w_ff1[l][:, fo * 128:(fo + 1) * 128], f"w1b{fo % 2}")
                    for tb in range(3):
                        cs = (tb * 512, (tb + 1) * 512)
                        bk, bkk = P.bank()
                        for kc in range(8):
                            P.mm(bk[:, :], wb.ap[:, kc, :], hT.ap[:, kc, cs[0]:cs[1]], start=(kc == 0), stop=(kc == 7),
                                 r=wb.K() + hT.K((kc, kc + 1), cs), w=[bkk])
                        rb = rl[(fo * 3 + tb) % 2]
                        P.act(rb.ap, bk[:, :], AF.Relu, r=[bkk], w=[rb])
                        P.tt("dve", hid.ap[:, fo, cs[0]:cs[1]], rb.ap, rb.ap, ALU.mult, r=[rb], w=hid.K((fo, fo + 1), cs))
                for mo in range(8):
                    wb = w2b[mo % 2]
                    load_w(wb, w_ff2[l][:, mo * 128:(mo + 1) * 128], f"w2b{mo % 2}")
                    for tb in range(3):
                        cs = (tb * 512, (tb + 1) * 512)
                        bk, bkk = P.bank()
                        for kc in range(32):
                            P.mm(bk[:, :], wb.ap[:, kc, :], hid.ap[:, kc, cs[0]:cs[1]], start=(kc == 0), stop=(kc == 31),
                                 r=wb.K((kc, kc + 1)) + hid.K((kc, kc + 1), cs), w=[bkk])
                        for s2 in range(2):
                            sg_ = tb * 2 + s2
                            c2 = (sg_ * 256, (sg_ + 1) * 256)
                            P.stt("dve", xT.ap[:, mo, c2[0]:c2[1]], bk[:, s2 * 256:(s2 + 1) * 256], modT.ap[:, 40 + mo, sg_:sg_ + 1], xT.ap[:, mo, c2[0]:c2[1]],
                                  ALU.mult, ALU.add, r=[bkk] + modT.K() + xT.K((mo, mo + 1), c2), w=xT.K((mo, mo + 1), c2))
                A.release(m_ff)
                P.dump(f"x{l}", xT)

            chk("layers")
            m_fin = A.mark()
            yn = A.alloc([8, 512], F32)
            yo = [A.alloc([D], F32) for _ in range(2)]
            for tb in rmsnorm(DEPTH - 1, None, 0, final=True, yn=yn):
                for it in range(4):
                    i = tb * 4 + it
                    ob = yo[i % 2]
                    for q in range(2):
                        bk, bkk = P.bank()
                        for cc in range(4):
                            c = q * 4 + cc
                            P.tr(bk[:, cc * 128:(cc + 1) * 128], yn.ap[:, c, it * 128:(it + 1) * 128], ident_f, r=yn.K((c, c + 1)) + cf.K(), w=[bkk])
                        P.cp("act" if q == 0 else "dve", ob.ap[:, q * 512:(q + 1) * 512], bk[:, :], r=[bkk], w=ob.K((q * 512, (q + 1) * 512)))
                    P.dma("sp", y_out[i * 128:(i + 1) * 128, :], ob.ap, f"yout{i % 2}", r=[ob])
            A.release(m_fin)
        except _Stop:
            pass
        P.S.final_wait("sp")
        P.S.emit(nc, st)
    return nc


POOL_WINDOWS = (2, 4, 8, 16)


def _win(n, w):
    t = np.arange(n)
    lo = np.clip(t - w // 2, 0, n)
    hi = np.clip(t + w - w // 2, 0, n)
    return lo, hi


def _consts():
    cf = np.zeros((128, NCF), np.float32)
    cf[:, CF["ident"]:CF["ident"] + 128] = np.eye(128)
    blk = np.zeros((128, 128), np.float32)
    blk[:64, :64] = 1
    blk[64:, 64:] = 1
    cf[:, CF["blk"]:CF["blk"] + 128] = blk
    cf[:, CF["ones"]:CF["ones"] + 128] = 1
    s = np.arange(128)[:, None] % 64
    t = np.arange(64)[None, :]
    cf[:, CF["mSU"]:CF["mSU"] + 64] = s < t
    cf[:, CF["mIU"]:CF["mIU"] + 64] = s <= t
    cf[:, CF["mSL"]:CF["mSL"] + 64] = s > t
    cf[:, CF["mIL"]:CF["mIL"] + 64] = s >= t
    cf[:, CF["idh"]:CF["idh"] + 64] = s == t
    for j in range(6):
        bs = 1 << j
        same_big = (s // (2 * bs)) == (t // (2 * bs))
        diff_small = (s // bs) != (t // bs)
        cf[:, CF["mU%d" % j]:CF["mU%d" % j] + 64] = same_big & diff_small & (s < t)
        cf[:, CF["mL%d" % j]:CF["mL%d" % j] + 64] = same_big & diff_small & (s > t)
    csm = np.ones((128, 256), np.float32)
    csm[:, ::64] = 0
    cf[:, CF["csm"]:CF["csm"] + 256] = csm
    cf[:, CF["eps"]] = 1e-6
    cf[:, CF["eps"] + 1] = 64e-5
    cb = np.zeros((128, 384), np.float32)
    cb[:, 0:128] = np.eye(128)
    cc = np.arange(64)
    ang = 2 * np.pi * np.outer(cc, cc) / 64
    for e in range(2):
        cb[e * 64:(e + 1) * 64, 128 + e * 64:128 + (e + 1) * 64] = np.cos(ang)
        cb[e * 64:(e + 1) * 64, 256 + e * 64:256 + (e + 1) * 64] = np.sin(ang)
    return cf, cb.astype(ml_dtypes.bfloat16)


def _dft(L):
    ll = np.arange(L)
    ang = 2 * np.pi * np.outer(ll, ll) / L
    sc = 1.0 / math.sqrt(L * 64)
    return np.cos(ang) * sc, -np.sin(ang) * sc


def _pool_1d(L):
    ms, cnts = [], []
    for w in POOL_WINDOWS:
        lo, hi = _win(L, w)
        l = np.arange(L)[:, None]
        m = ((l >= lo[None, :]) & (l < hi[None, :])).astype(np.float32)
        ms.append(m)
        cnts.append((hi - lo).astype(np.float32))
    return ms, cnts


def _pool_2d(rows, cols):
    ms, cnts = [], []
    for w in POOL_WINDOWS:
        lor, hir = _win(rows, w)
        loc, hic = _win(cols, w)
        r = np.arange(rows)[:, None]
        c = np.arange(cols)[:, None]
        mr = ((r >= lor[None, :]) & (r < hir[None, :])).astype(np.float32)
        mc = ((c >= loc[None, :]) & (c < hic[None, :])).astype(np.float32)
        m = np.einsum("ab,cd->acbd", mr, mc).reshape(rows * cols, rows * cols)
        cnt = np.outer((hir - lor), (hic - loc)).reshape(-1).astype(np.float32)
        ms.append(m)
        cnts.append(cnt)
    return ms, cnts


def _core_consts(is_sample):
    bf = ml_dtypes.bfloat16
    c256, s256 = _dft(256)
    dft_sm = np.stack([c256, s256]).astype(bf)
    m1, n1 = _pool_1d(256)
    pm_sm = np.stack(m1).astype(bf)
    if is_sample:
        c, s = _dft(1024)
        dft_big = np.stack([c, s]).astype(bf)
        m2, n2 = _pool_2d(16, 64)
        pm_big = np.stack(m2).astype(bf)
        cnt_big = n2
    else:
        dft_big = np.zeros((2, 1024, 1024), np.float32)
        pm_big = np.zeros((4, 1024, 1024), np.float32)
        for b in range(4):
            sl = slice(b * 256, (b + 1) * 256)
            dft_big[0, sl, sl] = c256
            dft_big[1, sl, sl] = s256
            for g in range(4):
                pm_big[g, sl, sl] = m1[g]
        dft_big = dft_big.astype(bf)
        pm_big = pm_big.astype(bf)
        cnt_big = [np.tile(n1[g], 4) for g in range(4)]
    icnt = np.zeros((128, 2, T), np.float32)
    for g in range(4):
        cnt = np.concatenate([cnt_big[g], n1[g], n1[g]])
        icnt[(g % 2) * 64:(g % 2 + 1) * 64, g // 2, :] = (1.0 / cnt)[None, :]
    return dict(dft_big=dft_big, dft_sm=dft_sm, pm_big=pm_big, pm_sm=pm_sm, icnt=icnt)


def _colmajor(v, n):
    return np.ascontiguousarray(np.asarray(v, np.float32).reshape(n, 128).T)


def _pack_pp(inp, l):
    pp = np.zeros((128, NPP), np.float32)
    pp[:, PP["n1g"]:PP["n1g"] + 8] = _colmajor(inp["norm1_g"][l], 8)
    pp[:, PP["mu"]:PP["mu"] + 15] = _colmajor(inp["mu_shift"][l], 15)
    pp[:, PP["w0"]:PP["w0"] + 8] = _colmajor(inp["w0"][l].reshape(-1), 8)
    pp[:, PP["a0"]:PP["a0"] + 8] = _colmajor(inp["a0"][l].reshape(-1), 8)
    pp[:, PP["kk"]:PP["kk"] + 4] = _colmajor(inp["k_k"][l], 4)
    pp[:, PP["ka"]:PP["ka"] + 4] = _colmajor(inp["k_a"][l], 4)
    pp[:, PP["rk"]:PP["rk"] + 4] = _colmajor(inp["r_k"][l].reshape(-1), 4)
    pp[:, PP["lnw"]:PP["lnw"] + 4] = _colmajor(inp["ln_x_w"][l], 4)
    pp[:, PP["lnb"]:PP["lnb"] + 4] = _colmajor(inp["ln_x_b"][l], 4)
    pp[:, PP["psc"]:PP["psc"] + 2] = _colmajor(inp["pool_scale"][l], 2)
    pp[:, PP["n2g"]:PP["n2g"] + 8] = _colmajor(inp["norm2_g"][l], 8)
    pp[:, PP["bada"]:PP["bada"] + 48] = _colmajor(inp["b_ada"][l], 48)
    pp[:, PP["fng"]:PP["fng"] + 8] = _colmajor(inp["final_norm_g"], 8)
    return pp


_NC_CACHE = {}


def _prep_inputs(inp):
    inp = {k: np.asarray(v) for k, v in inp.items()}
    cf, cb = _consts()
    pp = np.stack([_pack_pp(inp, l) for l in range(DEPTH)])
    shared = dict(
        pp=pp, cf=cf, cb=cb,
        w_ada=np.ascontiguousarray(inp["w_ada"], np.float32), w_in=np.ascontiguousarray(inp["w_in"], np.float32),
        w_up=np.ascontiguousarray(inp["w_up"].reshape(DEPTH, 128, 512), np.float32),
        a_up=np.ascontiguousarray(inp["a_up"].reshape(DEPTH, 128, 512), np.float32),
        g_up=np.ascontiguousarray(inp["g_up"], np.float32),
        w_fnet=np.ascontiguousarray(inp["w_fnet"], np.float32), w_pool=np.ascontiguousarray(inp["w_pool"], np.float32),
        w_out=np.ascontiguousarray(inp["w_out"], np.float32), w_ff1=np.ascontiguousarray(inp["w_ff1"], np.float32),
        w_ff2=np.ascontiguousarray(inp["w_ff2"], np.float32),
    )
    cc = {True: _core_consts(True), False: _core_consts(False)}
    xp, xs = inp["x_prompt"], inp["x_sample"]
    in_maps, plan = [], []
    for i in range(8):
        if i < 4:
            pb = [2 * i, 2 * i + 1]
            x = np.concatenate([xs[i], xp[pb[0]], xp[pb[1]]], 0)
            cond = np.stack([inp["c"][i]] * 4 + [inp["c_ctx"]] * 2, 1)
            sw = inp["state_wkv"][i]
            si = np.transpose(sw, (0, 1, 2, 4, 3)).reshape(DEPTH, 2, 4, 128, 64)
            cm = 1.0
            interior = [1, 2, 3]
            plan.append((i, pb, [4, 5]))
        else:
            j = i - 4
            pb = list(range(8 + 6 * j, 8 + 6 * j + 6))
            x = np.concatenate([xp[b] for b in pb], 0)
            cond = np.stack([inp["c_ctx"]] * 6, 1)
            si = np.zeros((DEPTH, 2, 4, 128, 64), np.float32)
            cm = 0.0
            interior = []
            plan.append((None, pb, list(range(6))))
        cmlr = np.zeros((128, 16), np.float32)
        cmlr[:, 0] = cm
        for b in range(1, NSEG):
            v = 1.0 if b in interior else 0.0
            cmlr[:, b] = v
            cmlr[:, 5 + b] = v
        m = dict(shared)
        m.update(cc[i < 4])
        m.update(x_in=np.ascontiguousarray(x, np.float32), cond=np.ascontiguousarray(cond, np.float32),
                 sinit=np.ascontiguousarray(si, np.float32), cmlr=cmlr)
        in_maps.append(m)
    return in_maps, plan


def kernel(**inp):
    debug = tuple(inp.pop("_debug", ()))
    in_maps, plan = _prep_inputs(inp)
    key = debug
    if key not in _NC_CACHE:
        _NC_CACHE[key] = build_program(debug)
    nc = _NC_CACHE[key]
    res = run_bass_kernel_spmd(nc, in_maps, core_ids=list(range(8)))
    y_prompt = np.zeros((32, 256, D), np.float32)
    y_sample = np.zeros((4, 1024, D), np.float32)
    new_state = np.zeros((32, DEPTH, 2, 8, 64, 64), np.float32)
    for i, (sb, pb, segs) in enumerate(plan):
        r = res.results[i]
        y = np.asarray(r["y_out"])
        so = np.asarray(r["st_out"])
        if sb is not None:
            y_sample[sb] = y[0:1024]
        for b, sgm in zip(pb, segs):
            y_prompt[b] = y[sgm * 256:(sgm + 1) * 256]
            new_state[b] = so[sgm]
    if debug:
        kernel.last_debug = [{k: np.asarray(v) for k, v in r.items() if k.startswith("dbg_")} for r in res.results]
    return (y_prompt, y_sample, new_state)
```

```python
import math
from contextlib import ExitStack

import numpy as np
import ml_dtypes
import concourse.bass as bass
import concourse.mybir as mybir
from concourse.bass_utils import run_bass_kernel_spmd

F32 = mybir.dt.float32
BF16 = mybir.dt.bfloat16
AF = mybir.ActivationFunctionType
ALU = mybir.AluOpType
AX = mybir.AxisListType

D = 1024
T = 1536
NSEG = 6
NCH = 24
DEPTH = 2
D_IN = 2432
C0 = math.exp(-0.5)
ENGS = ("pe", "act", "dve", "pool", "sp")
SEG = 2000
GR = 128

PP = {}
_o = 0
for _n, _w in (("n1g", 8), ("mu", 15), ("w0", 8), ("a0", 8), ("kk", 4), ("ka", 4), ("rk", 4), ("lnw", 4),
               ("lnb", 4), ("psc", 2), ("n2g", 8), ("bada", 48), ("fng", 8)):
    PP[_n] = _o
    _o += _w
NPP = _o
DP = {}
_o = 0
for _n, _w in (("omu", 15), ("hmu", 15), ("hw0", 8), ("ha0", 8), ("omka", 4)):
    DP[_n] = _o
    _o += _w
NDP = _o

CF = {}
_o = 0
for _n, _w in (("ident", 128), ("blk", 128), ("ones", 128), ("mSU", 64), ("mIU", 64), ("mSL", 64), ("mIL", 64),
               ("idh", 64), ("csm", 256), ("eps", 4),
               ("mU0", 64), ("mU1", 64), ("mU2", 64), ("mU3", 64), ("mU4", 64), ("mU5", 64),
               ("mL0", 64), ("mL1", 64), ("mL2", 64), ("mL3", 64), ("mL4", 64), ("mL5", 64)):
    CF[_n] = _o
    _o += _w
NCF = _o


class Sched:
    def __init__(self, same_engine_sync=False):
        self.q = {e: [] for e in ENGS}
        self.cnt = {e: 0 for e in ENGS}
        self.last_w = {}
        self.readers = {}
        self.waited = {e: {} for e in ENGS}
        self.dma_tot = {}
        self.same = same_engine_sync

    def _deps(self, eng, reads, writes):
        need = {}

        def nd(tok):
            if tok is None:
                return
            k, v = tok
            if k[0] == "dma":
                v = self.dma_tot[k[1]]
            if need.get(k, 0) < v:
                need[k] = v

        for k in reads:
            nd(self.last_w.get(k))
        for k in writes:
            nd(self.last_w.get(k))
            rd = self.readers.get(k)
            if rd:
                for kk, v in rd.items():
                    nd((kk, v))
        waits = []
        for k, v in need.items():
            if k[0] == "eng" and k[1] == eng and (not self.same or eng in ("pe", "sp")):
                continue
            if self.waited[eng].get(k, 0) >= v:
                continue
            self.waited[eng][k] = v
            waits.append((k, v))
        return waits

    def _record(self, tok, reads, writes):
        for k in reads:
            rd = self.readers.setdefault(k, {})
            if rd.get(tok[0], 0) < tok[1]:
                rd[tok[0]] = tok[1]
        for k in writes:
            self.last_w[k] = tok
            self.readers[k] = {}

    def op(self, eng, fn, reads=(), writes=()):
        waits = self._deps(eng, reads, writes)
        idx = self.cnt[eng]
        self.cnt[eng] += 1
        self.q[eng].append(("op", fn, waits, idx))
        self._record((("eng", eng), idx + 1), reads, writes)

    def dma(self, eng, fn, semkey, reads=(), writes=()):
        waits = self._deps(eng, reads, writes)
        tot = self.dma_tot.get(semkey, 0) + 16
        self.dma_tot[semkey] = tot
        self.q[eng].append(("dma", fn, waits, semkey))
        self._record((("dma", semkey), tot), reads, writes)

    def final_wait(self, eng):
        waits = [(("dma", k), v) for k, v in self.dma_tot.items()]
        self.q[eng].append(("wait", None, waits, None))

    def emit(self, nc, stack):
        sems = {}
        for e in ENGS:
            if e == "sp":
                continue
            for s in range(self.cnt[e] // SEG + 1):
                sems[("eng", e, s)] = stack.enter_context(nc.semaphore(f"p_{e}_{s}"))
        for semkey in self.dma_tot:
            sems[("dma", semkey)] = stack.enter_context(nc.semaphore(f"d_{semkey}"))
        block = stack.enter_context(nc.Block())

        def run(ename):
            def body(engine):
                for kind, fn, waits, aux in self.q[ename]:
                    for k, v in waits:
                        if k[0] == "dma":
                            engine.wait_ge(sems[k], v)
                        else:
                            s = (v - 1) // SEG
                            engine.wait_ge(sems[("eng", k[1], s)], v - s * SEG)
                    if kind == "op":
                        fn(engine).then_inc(sems[("eng", ename, aux // SEG)], 1)
                    elif kind == "dma":
                        fn(engine).then_inc(sems[("dma", aux)], 16)
            return body

        block.tensor(run("pe"))
        block.scalar(run("act"))
        block.vector(run("dve"))
        block.gpsimd(run("pool"))
        block.sync(run("sp"))


class Buf:
    def __init__(self, arena, off, shape, dt):
        n = int(np.prod(shape))
        self.esz = 2 if dt == BF16 else 4
        words = (n * self.esz + 3) // 4
        self.off, self.words, self.shape, self.dt, self.n = off, words, tuple(shape), dt, n
        base = arena[:, off:off + words]
        if dt == BF16:
            base = base.bitcast(BF16)
        self.flat = base
        if len(shape) == 1:
            self.ap = base
        elif len(shape) == 2:
            self.ap = base.rearrange("p (a b) -> p a b", a=shape[0])
        elif len(shape) == 3:
            self.ap = base.rearrange("p (a b c) -> p a b c", a=shape[0], b=shape[1])
        else:
            raise ValueError(shape)

    def K(self, *idx):
        idx = list(idx) + [None] * (len(self.shape) - len(idx))
        box = [(0, s) if i is None else i for i, s in zip(idx, self.shape)]
        strides = [int(np.prod(self.shape[i + 1:])) for i in range(len(self.shape))]
        lead = box[:-1]
        nlead = int(np.prod([hi - lo for lo, hi in lead])) if lead else 1
        ranges = []
        if nlead <= 64:
            def rec(d, base):
                if d == len(self.shape) - 1:
                    ranges.append((base + box[d][0], base + box[d][1]))
                    return
                for i in range(box[d][0], box[d][1]):
                    rec(d + 1, base + i * strides[d])
            rec(0, 0)
        else:
            lo = sum(b[0] * s for b, s in zip(box, strides))
            hi = sum((b[1] - 1) * s for b, s in zip(box, strides)) + 1
            ranges.append((lo, hi))
        keys = set()
        for lo, hi in ranges:
            w0 = self.off + (lo * self.esz) // 4
            w1 = self.off + (hi * self.esz + 3) // 4
            for g in range(w0 // GR, (w1 - 1) // GR + 1):
                keys.add(("g", g))
        return list(keys)


class Arena:
    def __init__(self, nc, stack, words):
        self.t = stack.enter_context(nc.sbuf_tensor("arena", [128, words], F32))
        self.words = words
        self.top = 0

    def alloc(self, shape, dt):
        self.top = (self.top + GR - 1) // GR * GR
        b = Buf(self.t, self.top, shape, dt)
        self.top += b.words
        self.hw = max(getattr(self, 'hw', 0), self.top)
        assert self.top <= self.words, (self.top, self.words)
        return b

    def mark(self):
        return self.top

    def release(self, m):
        self.top = m


def _keys(xs):
    out = []
    for x in xs:
        if isinstance(x, Buf):
            out.extend(x.K())
        elif isinstance(x, list):
            out.extend(x)
        else:
            out.append(x)
    return out


class Prog:
    def __init__(self, nc, stack, debug=()):
        self.nc = nc
        self.st = stack
        self.S = Sched(True)
        self.debug = set(debug)
        self.ps = [stack.enter_context(nc.psum_tensor(f"ps{i}", [128, 512], F32)) for i in range(8)]
        self.psk = [("ps", i) for i in range(8)]
        self.bank_i = 0
        self.nw = 0

    def bank(self):
        i = self.bank_i
        self.bank_i = (i + 1) % 8
        return self.ps[i], self.psk[i]

    def mm(self, out, lhsT, rhs, start=True, stop=True, r=(), w=()):
        self.S.op("pe", lambda e: e.matmul(out, lhsT=lhsT, rhs=rhs, start=start, stop=stop), _keys(r), _keys(w))

    def tr(self, out, in_, ident, r=(), w=()):
        self.S.op("pe", lambda e: e.transpose(out, in_, ident), _keys(r), _keys(w))

    def act(self, out, in_, func, bias=None, scale=None, r=(), w=()):
        kw = {}
        if bias is not None:
            kw["bias"] = bias
        if scale is not None:
            kw["scale"] = scale
        self.S.op("act", lambda e: e.activation(out=out, in_=in_, func=func, **kw), _keys(r), _keys(w))

    def cp(self, eng, out, in_, r=(), w=()):
        if eng == "act":
            self.S.op("act", lambda e: e.copy(out=out, in_=in_), _keys(r), _keys(w))
        else:
            self.S.op(eng, lambda e: e.tensor_copy(out=out, in_=in_), _keys(r), _keys(w))

    def tt(self, eng, out, in0, in1, op, r=(), w=()):
        self.S.op(eng, lambda e: e.tensor_tensor(out=out, in0=in0, in1=in1, op=op), _keys(r), _keys(w))

    def ts(self, eng, out, in0, s1, s2, op0, op1=None, r=(), w=()):
        if op1 is None:
            self.S.op(eng, lambda e: e.tensor_scalar(out=out, in0=in0, scalar1=s1, scalar2=None, op0=op0), _keys(r), _keys(w))
        else:
            self.S.op(eng, lambda e: e.tensor_scalar(out=out, in0=in0, scalar1=s1, scalar2=s2, op0=op0, op1=op1), _keys(r), _keys(w))

    def stt(self, eng, out, in0, scalar, in1, op0, op1, r=(), w=()):
        self.S.op(eng, lambda e: e.scalar_tensor_tensor(out=out, in0=in0, scalar=scalar, in1=in1, op0=op0, op1=op1), _keys(r), _keys(w))

    def recip(self, out, in_, r=(), w=()):
        self.S.op("dve", lambda e: e.reciprocal(out=out, in_=in_), _keys(r), _keys(w))

    def memset(self, eng, ap, val, w=()):
        self.S.op(eng, lambda e: e.memset(ap, val), (), _keys(w))

    def dma(self, eng, out, in_, sem, r=(), w=()):
        self.S.dma(eng, lambda e: e.dma_start(out=out, in_=in_), sem, _keys(r), _keys(w))

    def dump(self, name, buf):
        if name not in self.debug:
            return
        shape = [128] + list(buf.shape)
        d = self.nc.dram_tensor("dbg_" + name, shape, F32 if buf.dt == F32 else BF16, kind="ExternalOutput").ap()
        if len(buf.shape) >= 2 and buf.words > 2048:
            for a in range(buf.shape[0]):
                self.dma("sp", d[:, a], buf.ap[:, a], "dbg", r=[buf])
        else:
            self.dma("sp", d, buf.ap, "dbg", r=[buf])


def v3(ap, a):
    return ap.rearrange("p (a b) -> p a b", a=a)


class _Stop(Exception):
    pass


def build_program(debug=(), stop=None):
    def chk(name):
        if stop == name:
            raise _Stop()

    nc = bass.Bass("TRN2", target_bir_lowering=False)
    din = lambda n, s, dt=F32: nc.dram_tensor(n, list(s), dt, kind="ExternalInput").ap()
    x_in = din("x_in", [T, D])
    cond = din("cond", [D, NSEG])
    sinit = din("sinit", [DEPTH, 2, 4, 128, 64])
    cmlr = din("cmlr", [128, 16])
    pp_d = din("pp", [DEPTH, 128, NPP])
    cf_d = din("cf", [128, NCF])
    cb_d = din("cb", [128, 384], BF16)
    w_ada = din("w_ada", [DEPTH, D, 6 * D])
    w_in = din("w_in", [DEPTH, D, D_IN])
    w_up = din("w_up", [DEPTH, 128, 512])
    a_up = din("a_up", [DEPTH, 128, 512])
    g_up = din("g_up", [DEPTH, 128, 512])
    w_fnet = din("w_fnet", [DEPTH, 4, 64, 64])
    w_pool = din("w_pool", [DEPTH, 4, 64, 64])
    w_out = din("w_out", [DEPTH, D, D])
    w_ff1 = din("w_ff1", [DEPTH, D, 4 * D])
    w_ff2 = din("w_ff2", [DEPTH, 4 * D, D])
    dft_big = din("dft_big", [2, 1024, 1024], BF16)
    dft_sm = din("dft_sm", [2, 256, 256], BF16)
    pm_big = din("pm_big", [4, 1024, 1024], BF16)
    pm_sm = din("pm_sm", [4, 256, 256], BF16)
    icnt_d = din("icnt", [128, 2, T])
    y_out = nc.dram_tensor("y_out", [T, D], F32, kind="ExternalOutput").ap()
    st_out = nc.dram_tensor("st_out", [NSEG, DEPTH, 2, 8, 64, 64], F32, kind="ExternalOutput").ap()

    with ExitStack() as st:
        P = Prog(nc, st, debug)
        A = Arena(nc, st, 53200)
        xT = A.alloc([8, T], F32)
        hT = A.alloc([8, T], BF16)
        modT = A.alloc([48, NSEG], F32)
        A1 = A.alloc([8, NSEG], F32)
        A2 = A.alloc([8, NSEG], F32)
        pp = [A.alloc([NPP], F32) for _ in range(DEPTH)]
        dp = [A.alloc([NDP], F32) for _ in range(DEPTH)]
        cf = A.alloc([NCF], F32)
        cb = A.alloc([384], BF16)
        condT = A.alloc([8, NSEG], F32)
        scond = A.alloc([8, NSEG], BF16)
        cm = A.alloc([16], F32)
        OV = A.mark()

        cfa = cf.ap
        ident_f = cfa[:, CF["ident"]:CF["ident"] + 128]
        blk_f = cfa[:, CF["blk"]:CF["blk"] + 128]
        ones_f = cfa[:, CF["ones"]:CF["ones"] + 128]
        idh = cfa[:, CF["idh"]:CF["idh"] + 64]
        csm = cfa[:, CF["csm"]:CF["csm"] + 256]
        eps_rms = cfa[:, CF["eps"]:CF["eps"] + 1]
        eps_gn = cfa[:, CF["eps"] + 1:CF["eps"] + 2]
        hc0 = cfa[:, CF["eps"] + 2:CF["eps"] + 3]
        half_c = cfa[:, CF["eps"] + 3:CF["eps"] + 4]

        def mask(name):
            return cfa[:, CF[name]:CF[name] + 64].unsqueeze(1).to_broadcast([128, 4, 64])

        idh4 = idh.unsqueeze(1).to_broadcast([128, 4, 64])
        ident_b = cb.ap[:, 0:128]
        dftphi = cb.ap[:, 128:384]

        def ppc(l, name, i=0, n=1):
            return pp[l].ap[:, PP[name] + i:PP[name] + i + n]

        def dpc(l, name, i=0, n=1):
            return dp[l].ap[:, DP[name] + i:DP[name] + i + n]

        P.dma("sp", cf.ap, cf_d, "cst", w=[cf])
        P.dma("sp", cb.ap, cb_d, "cst", w=[cb])
        P.dma("sp", cm.ap, cmlr, "cst", w=[cm])
        for l in range(DEPTH):
            P.dma("sp", pp[l].ap, pp_d[l], "cst", w=[pp[l]])
        P.dma("sp", condT.ap, cond.rearrange("(c p) s -> p c s", p=128), "cst", w=[condT])
        P.act(scond.ap, condT.ap, AF.Silu, r=[condT], w=[scond])
        for l in range(DEPTH):
            P.ts("dve", dpc(l, "omu", 0, 15), ppc(l, "mu", 0, 15), -1.0, 1.0, ALU.mult, ALU.add, r=[pp[l]], w=[dp[l]])
            P.ts("dve", dpc(l, "hmu", 0, 15), ppc(l, "mu", 0, 15), 0.5, None, ALU.mult, r=[pp[l]], w=[dp[l]])
            P.ts("dve", dpc(l, "hw0", 0, 8), ppc(l, "w0", 0, 8), 0.5, None, ALU.mult, r=[pp[l]], w=[dp[l]])
            P.ts("dve", dpc(l, "ha0", 0, 8), ppc(l, "a0", 0, 8), 0.5, None, ALU.mult, r=[pp[l]], w=[dp[l]])
            P.ts("dve", dpc(l, "omka", 0, 4), ppc(l, "ka", 0, 4), -1.0, 1.0, ALU.mult, ALU.add, r=[pp[l]], w=[dp[l]])

        try:
            m0 = A.mark()
            xin = [A.alloc([D], F32) for _ in range(2)]
            for i in range(12):
                xb = xin[i % 2]
                P.dma("sp", xb.ap, x_in[i * 128:(i + 1) * 128, :], f"xin{i % 2}", w=[xb])
                for q in range(2):
                    bk, bkk = P.bank()
                    for cc in range(4):
                        c = q * 4 + cc
                        P.tr(bk[:, cc * 128:(cc + 1) * 128], xb.ap[:, c * 128:(c + 1) * 128], ident_f, r=[xb, cf], w=[bkk])
                    P.cp("act" if q == 0 else "dve", xT.ap[:, q * 4:(q + 1) * 4, i * 128:(i + 1) * 128], v3(bk[:, :], 4),
                         r=[bkk], w=xT.K((q * 4, q * 4 + 4), (i * 128, (i + 1) * 128)))
            A.release(m0)

            chk("x")
            def rmsnorm(l, Acoef, shift0, final=False, yn=None):
                m = A.mark()
                sqb = [A.alloc([512], F32) for _ in range(2)]
                rs = A.alloc([512], F32)
                xnb = [A.alloc([512], F32) for _ in range(2)]
                for tb in range(3):
                    cs = (tb * 512, (tb + 1) * 512)
                    bk, bkk = P.bank()
                    for c in range(8):
                        sq = sqb[c % 2]
                        P.act(sq.ap, xT.ap[:, c, cs[0]:cs[1]], AF.Square, r=xT.K((c, c + 1), cs), w=[sq])
                        P.mm(bk[:, :], ones_f, sq.ap, start=(c == 0), stop=(c == 7), r=[sq, cf], w=[bkk])
                    P.act(rs.ap, bk[:, :], AF.Sqrt, bias=eps_rms, scale=1.0 / D, r=[bkk, cf], w=[rs])
                    P.recip(rs.ap, rs.ap, r=[rs], w=[rs])
                    for c in range(8):
                        if final:
                            P.tt("dve", yn.ap[:, c, :], xT.ap[:, c, cs[0]:cs[1]], rs.ap, ALU.mult, r=xT.K((c, c + 1), cs) + rs.K(), w=yn.K((c, c + 1)))
                            P.act(yn.ap[:, c, :], yn.ap[:, c, :], AF.Identity, scale=ppc(l, "fng", c), r=yn.K((c, c + 1)) + pp[l].K(), w=yn.K((c, c + 1)))
                            continue
                        xn = xnb[c % 2]
                        P.tt("dve", xn.ap, xT.ap[:, c, cs[0]:cs[1]], rs.ap, ALU.mult, r=xT.K((c, c + 1), cs) + rs.K(), w=[xn])
                        for s2 in range(2):
                            sg_ = tb * 2 + s2
                            P.act(hT.ap[:, c, sg_ * 256:(sg_ + 1) * 256], xn.ap[:, s2 * 256:(s2 + 1) * 256], AF.Identity,
                                  bias=modT.ap[:, shift0 + c, sg_:sg_ + 1], scale=Acoef.ap[:, c, sg_:sg_ + 1],
                                  r=xn.K() + modT.K() + Acoef.K(), w=hT.K((c, c + 1), (sg_ * 256, (sg_ + 1) * 256)))
                    if final:
                        yield tb
                A.release(m)

            def run(gen):
                for _ in gen:
                    pass

            def load_w(dst, src_ap, sem):
                P.dma("pool", dst.ap, src_ap.rearrange("(k p) n -> p k n", p=128), sem, w=[dst])

            wcnt = [0]

            def proj_chunk(l, j, wbufs, dst, dst2=None, sc2=None, func=None):
                wb = wbufs[wcnt[0] % 2]
                load_w(wb, w_in[l][:, j * 128:(j + 1) * 128], f"wsm{wcnt[0] % 2}")
                wcnt[0] += 1
                for tb in range(3):
                    cs = (tb * 512, (tb + 1) * 512)
                    bk, bkk = P.bank()
                    for kc in range(8):
                        P.mm(bk[:, :], wb.ap[:, kc, :], hT.ap[:, kc, cs[0]:cs[1]], start=(kc == 0), stop=(kc == 7),
                             r=wb.K() + hT.K((kc, kc + 1), cs), w=[bkk])
                    if func is None:
                        P.cp("act", dst.ap[:, cs[0]:cs[1]], bk[:, :], r=[bkk], w=dst.K(cs))
                    else:
                        P.act(dst.ap[:, cs[0]:cs[1]], bk[:, :], func, r=[bkk], w=dst.K(cs))
                    if dst2 is not None:
                        P.act(dst2.ap[:, cs[0]:cs[1]], bk[:, :], AF.Identity, scale=sc2, r=[bkk], w=dst2.K(cs))

            def shift(l, j, p, p2, nb, out):
                P.tt("dve", nb.ap[:, 1:T - 1], p.ap[:, 0:T - 2], p.ap[:, 2:T], ALU.add, r=[p], w=[nb])
                P.cp("dve", nb.ap[:, 0:1], p.ap[:, 1:2], r=[p], w=[nb])
                P.cp("dve", nb.ap[:, T - 1:T], p.ap[:, T - 2:T - 1], r=[p], w=[nb])
                for b in range(1, NSEG):
                    t = 256 * b
                    P.stt("dve", nb.ap[:, t:t + 1], p.ap[:, t - 1:t], cm.ap[:, b:b + 1], p.ap[:, t + 1:t + 2], ALU.mult, ALU.add, r=[p, cm], w=[nb])
                    P.stt("dve", nb.ap[:, t - 1:t], p.ap[:, t:t + 1], cm.ap[:, 5 + b:6 + b], p.ap[:, t - 2:t - 1], ALU.mult, ALU.add, r=[p, cm], w=[nb])
                P.stt("dve", out.ap, nb.ap, dpc(l, "hmu", j), p2.ap, ALU.mult, ALU.add, r=[nb, p2, dp[l]], w=[out])

            for l in range(DEPTH):
                m_l = A.mark()
                wada = [A.alloc([8, 512], BF16) for _ in range(2)]
                for nb_ in range(12):
                    wb = wada[nb_ % 2]
                    load_w(wb, w_ada[l][:, nb_ * 512:(nb_ + 1) * 512], f"wada{nb_ % 2}")
                    bk, bkk = P.bank()
                    for m in range(4):
                        for kc in range(8):
                            P.mm(bk[:, m * 8:m * 8 + 6], wb.ap[:, kc, m * 128:(m + 1) * 128], scond.ap[:, kc, :],
                                 start=(kc == 0), stop=(kc == 7), r=[wb, scond], w=[bkk])
                    for m in range(4):
                        ch = nb_ * 4 + m
                        P.act(modT.ap[:, ch, :], bk[:, m * 8:m * 8 + 6], AF.Identity, bias=ppc(l, "bada", ch), scale=1.0,
                              r=[bkk, pp[l]], w=modT.K((ch, ch + 1)))
                A.release(m_l)
                for c0_ in (8, 32):
                    P.ts("dve", modT.ap[:, c0_:c0_ + 8, :], modT.ap[:, c0_:c0_ + 8, :], 1.0, None, ALU.add, r=[modT], w=[modT])
                for c in range(8):
                    P.ts("dve", A1.ap[:, c, :], modT.ap[:, 8 + c, :], ppc(l, "n1g", c), None, ALU.mult, r=[modT, pp[l]], w=[A1])
                    P.ts("dve", A2.ap[:, c, :], modT.ap[:, 32 + c, :], ppc(l, "n2g", c), None, ALU.mult, r=[modT, pp[l]], w=[A2])
                P.dump(f"mod{l}", modT)

                chk("mod")
                run(rmsnorm(l, A1, 0))
                P.dump(f"h{l}", hT)

                chk("norm1")
                m_mix = A.mark()
                mixedT = A.alloc([8, T], BF16)
                tw = A.alloc([T], BF16)
                ad = A.alloc([T], BF16)
                sg = A.alloc([T], BF16)
                wsm = [A.alloc([8, 128], BF16) for _ in range(2)]
                wup = A.alloc([512], BF16)
                aup = A.alloc([512], BF16)
                gup = A.alloc([512], BF16)
                P.dma("pool", wup.ap, w_up[l], "wlora", w=[wup])
                P.dma("pool", aup.ap, a_up[l], "wlora", w=[aup])
                P.dma("pool", gup.ap, g_up[l], "wlora", w=[gup])
                m_rw = A.mark()
                R = A.alloc([T], F32)
                Kb = A.alloc([T], F32)
                V = A.alloc([T], F32)
                KK = A.alloc([T], F32)
                BS = A.alloc([T], F32)
                VT = A.alloc([NCH, 64], BF16)
                YS = A.alloc([NCH, 64], F32)
                McT = A.alloc([NCH, 64], BF16)
                Rh = A.alloc([NCH, 64], BF16)
                Gc = A.alloc([NCH, 64], BF16)
                YvT = A.alloc([NCH, 64], BF16)
                STb = A.alloc([64], BF16)
                SI = A.alloc([64], F32)
                STO = A.alloc([NSEG, 64], F32)
                STOt = Buf(A.t, YvT.off, [NSEG, 128], F32)
                MV = A.alloc([2, NCH], F32)
                m_seg = A.mark()
                Pp = A.alloc([T], F32)
                P2 = A.alloc([T], F32)
                NB = A.alloc([T], F32)
                A.release(m_seg)
                class _NS:
                    pass

                def mk_temps(balloc):
                    t_ = _NS()
                    for nm_ in ("SGt", "ASt", "INC", "EXC", "REM", "REMI", "KD", "Bv"):
                        setattr(t_, nm_, A.alloc([256], F32))
                    t_.TMP = t_.SGt
                    t_.E2 = t_.ASt
                    t_.WC = A.alloc([4], F32)
                    for nm_, shp in (("AR", [4, 2, 64]), ("Bt", [4, 64]), ("Kt", [4, 64]), ("BKh", [2, 256]), ("AXb", [4, 2, 64]),
                                     ("BA", [4, 2, 64]), ("KHT", [4, 64]), ("TT", [4, 64]), ("TTt", [4, 64]), ("XB", [4, 64]),
                                     ("NM", [2, 4, 64]), ("AAK", [4, 64]), ("AKR", [4, 64]), ("AU", [4, 2, 64])):
                        setattr(t_, nm_, balloc(shp, BF16))
                    return t_

                TA = mk_temps(A.alloc)
                _bo = [mixedT.off + 4 * T // 2]

                def _balloc_b(shp, dt):
                    bb_ = Buf(A.t, _bo[0], shp, dt)
                    _bo[0] += (bb_.words + GR - 1) // GR * GR
                    assert _bo[0] <= mixedT.off + mixedT.words
                    return bb_

                TB = mk_temps(_balloc_b)
                A.top = max(A.top, NB.off + T)
                m_rw_end = A.mark()

                def w3(b):
                    return b.flat.rearrange("p (c x) -> p c x", c=4)

                for j, dstb, fn, sc in ((12, tw, AF.Tanh, 1.0), (13, ad, AF.Identity, 1.0), (14, sg, AF.Tanh, 0.5)):
                    proj_chunk(l, j, wsm, Pp, P2, dpc(l, "omu", j))
                    shift(l, j, Pp, P2, NB, P2)
                    if j == 14:
                        P.act(P2.ap, P2.ap, AF.Tanh, scale=0.5, r=[P2], w=[P2])
                        P.ts("dve", sg.ap, P2.ap, 0.5, 0.5, ALU.mult, ALU.add, r=[P2], w=[sg])
                    else:
                        P.act(dstb.ap, P2.ap, fn, r=[P2], w=[dstb])

                chk("lora")
                for p in range(4):
                    h_ = [slice(0, 64), slice(64, 128)]
                    for j, dstb in ((p, R), (4 + p, Kb), (8 + p, V)):
                        proj_chunk(l, j, wsm, Pp, P2, dpc(l, "omu", j))
                        shift(l, j, Pp, P2, NB, dstb)
                    if p == 0:
                        P.dump(f"r{l}", R); P.dump(f"k{l}", Kb); P.dump(f"v{l}", V)
                    chk("rkv")
                    P.ts("dve", KK.ap, Kb.ap, ppc(l, "kk", p), None, ALU.mult, r=[Kb, pp[l]], w=[KK])
                    for tb in range(3):
                        cs = (tb * 512, (tb + 1) * 512)
                        P.act(Pp.ap[:, cs[0]:cs[1]], KK.ap[:, cs[0]:cs[1]], AF.Square, r=KK.K(cs), w=Pp.K(cs))
                        bk, bkk = P.bank()
                        P.mm(bk[:, :], blk_f, Pp.ap[:, cs[0]:cs[1]], r=Pp.K(cs) + cf.K(), w=[bkk])
                        P.act(P2.ap[:, cs[0]:cs[1]], bk[:, :], AF.Sqrt, r=[bkk], w=P2.K(cs))
                    P.ts("dve", P2.ap, P2.ap, 1e-12, None, ALU.max, r=[P2], w=[P2])
                    P.recip(P2.ap, P2.ap, r=[P2], w=[P2])
                    P.tt("dve", KK.ap, KK.ap, P2.ap, ALU.mult, r=[KK, P2], w=[KK])
                    chk("kk")
                    for sgm in range(NSEG):
                        bk, bkk = P.bank()
                        for c4 in range(4):
                            ch = sgm * 4 + c4
                            for e in range(2):
                                P.mm(bk[h_[e], c4 * 64:(c4 + 1) * 64], V.ap[h_[e], ch * 64:(ch + 1) * 64], ident_f[h_[e], h_[e]],
                                     r=V.K((ch * 64, (ch + 1) * 64)) + cf.K(), w=[bkk])
                        P.cp("act", VT.ap[:, sgm * 4:(sgm + 1) * 4, :], v3(bk[:, 0:256], 4), r=[bkk], w=VT.K((sgm * 4, (sgm + 1) * 4)))

                    chk("vt")
                    for d in range(2):
                        dh = slice(d * 64, (d + 1) * 64)
                        mS = mask("mSU" if d == 0 else "mSL")
                        mI = mask("mIU" if d == 0 else "mIL")
                        mST = mask("mSL" if d == 0 else "mSU")
                        def seg_body(Tm, bankf, sgm):
                            SGt, ASt, INC, EXC, REM, REMI, KD, Bv, TMP, E2, WC, AR, Bt, Kt, BKh, AXb, BA, KHT, TT, TTt, XB, NM, AAK, AKR, AU = (Tm.SGt, Tm.ASt, Tm.INC, Tm.EXC, Tm.REM, Tm.REMI, Tm.KD, Tm.Bv, Tm.TMP, Tm.E2, Tm.WC, Tm.AR, Tm.Bt, Tm.Kt, Tm.BKh, Tm.AXb, Tm.BA, Tm.KHT, Tm.TT, Tm.TTt, Tm.XB, Tm.NM, Tm.AAK, Tm.AKR, Tm.AU)
                            cs = (sgm * 256, (sgm + 1) * 256)
                            csl = slice(cs[0], cs[1])
                            bk, bkk = bankf()
                            P.mm(bk[:, 0:256], wup.ap[dh, p * 128:(p + 1) * 128], tw.ap[dh, csl], r=[wup] + tw.K(cs), w=[bkk])
                            P.mm(bk[:, 256:512], aup.ap[dh, p * 128:(p + 1) * 128], ad.ap[dh, csl], r=[aup] + ad.K(cs), w=[bkk])
                            P.act(SGt.ap, bk[:, 0:256], AF.Tanh, bias=dpc(l, "hw0", d * 4 + p), scale=0.5, r=[bkk, dp[l]], w=[SGt])
                            P.act(ASt.ap, bk[:, 256:512], AF.Tanh, bias=dpc(l, "ha0", d * 4 + p), scale=0.5, r=[bkk, dp[l]], w=[ASt])
                            P.act(SGt.ap, SGt.ap, AF.Identity, bias=hc0, scale=0.5 * C0, r=[SGt, cf], w=[SGt])
                            P.act(ASt.ap, ASt.ap, AF.Identity, bias=half_c, scale=0.5, r=[ASt, cf], w=[ASt])
                            yield
                            P.S.op("dve", lambda e, o=INC.ap, d0=csm, d1=SGt.ap: e.tensor_tensor_scan(out=o, data0=d0, data1=d1, initial=0.0, op0=ALU.mult, op1=ALU.add),
                                   _keys([SGt, cf]), _keys([INC]))
                            inc3 = v3(INC.ap, 4)
                            P.tt("dve", EXC.ap, INC.ap, SGt.ap, ALU.subtract, r=[INC, SGt], w=[EXC])
                            P.tt("dve", v3(REM.ap, 4), inc3[:, :, 63:64].to_broadcast([128, 4, 64]), inc3, ALU.subtract, r=[INC], w=[REM])
                            P.tt("dve", REMI.ap, REM.ap, SGt.ap, ALU.add, r=[REM, SGt], w=[REMI])
                            P.act(WC.ap, inc3[:, :, 63], AF.Exp, scale=-1.0, r=[INC], w=[WC])
                            cI, cE, cR = (INC, EXC, REM) if d == 0 else (REMI, REM, EXC)
                            yield
                            P.act(TMP.ap, ASt.ap, AF.Identity, bias=dpc(l, "omka", p), scale=ppc(l, "ka", p), r=[ASt, pp[l], dp[l]], w=[TMP])
                            P.tt("dve", KD.ap, TMP.ap, Kb.ap[:, csl], ALU.mult, r=TMP.K() + Kb.K(cs), w=[KD])
                            P.tt("dve", Bv.ap, KK.ap[:, csl], ASt.ap, ALU.mult, r=KK.K(cs) + ASt.K(), w=[Bv])
                            if d == 0:
                                P.stt("dve", BS.ap[:, csl], R.ap[:, csl], ppc(l, "rk", p), KD.ap, ALU.mult, ALU.mult, r=R.K(cs) + KD.K() + pp[l].K(), w=BS.K(cs))
                            else:
                                P.stt("dve", TMP.ap, R.ap[:, csl], ppc(l, "rk", p), KD.ap, ALU.mult, ALU.mult, r=R.K(cs) + KD.K() + pp[l].K(), w=[TMP])
                                P.tt("dve", BS.ap[:, csl], BS.ap[:, csl], TMP.ap, ALU.add, r=BS.K(cs) + TMP.K(), w=BS.K(cs))
                            yield
                            P.act(E2.ap, cI.ap, AF.Exp, scale=-1.0, r=[cI], w=[E2])
                            P.tt("dve", AR.ap[:, :, 1, :], v3(R.ap[:, csl], 4), v3(E2.ap, 4), ALU.mult, r=R.K(cs) + E2.K(), w=[AR])
                            P.act(cI.ap, cI.ap, AF.Exp, scale=1.0, r=[cI], w=[cI])
                            P.tt("dve", Bt.ap, v3(Bv.ap, 4), v3(cI.ap, 4), ALU.mult, r=[Bv, cI], w=[Bt])
                            P.tt("dve", Kt.ap, v3(KD.ap, 4), v3(cI.ap, 4), ALU.mult, r=[KD, cI], w=[Kt])
                            P.act(cE.ap, cE.ap, AF.Exp, scale=-1.0, r=[cE], w=[cE])
                            P.stt("dve", AR.ap[:, :, 0, :], v3(KK.ap[:, csl], 4), -1.0, v3(cE.ap, 4), ALU.mult, ALU.mult, r=KK.K(cs) + cE.K(), w=[AR])
                            P.act(cR.ap, cR.ap, AF.Exp, scale=-1.0, r=[cR], w=[cR])
                            P.tt("dve", BKh.ap[:, 0, :], Bv.ap, cR.ap, ALU.mult, r=[Bv, cR], w=[BKh])
                            P.tt("dve", BKh.ap[:, 1, :], KD.ap, cR.ap, ALU.mult, r=[KD, cR], w=[BKh])
                            yield
                            bk, bkk = bankf()
                            bb = bk[:, :].bitcast(BF16)
                            for c4 in range(4):
                                for e in range(2):
                                    idb = ident_b[h_[e], h_[e]]
                                    P.tr(bb[h_[e], c4 * 64:(c4 + 1) * 64], AR.ap[h_[e], c4, 0, :], idb, r=[AR, cb], w=[bkk])
                                    P.tr(bb[h_[e], 256 + c4 * 64:256 + (c4 + 1) * 64], BKh.ap[h_[e], 0, c4 * 64:(c4 + 1) * 64], idb, r=[BKh, cb], w=[bkk])
                                    P.tr(bb[h_[e], 512 + c4 * 64:512 + (c4 + 1) * 64], BKh.ap[h_[e], 1, c4 * 64:(c4 + 1) * 64], idb, r=[BKh, cb], w=[bkk])
                            P.cp("act", AXb.ap[:, :, 0, :], v3(bb[:, 0:256], 4), r=[bkk], w=[AXb])
                            P.cp("act", BA.ap[:, :, 0, :], v3(bb[:, 256:512], 4), r=[bkk], w=[BA])
                            P.cp("act", KHT.ap, v3(bb[:, 512:768], 4), r=[bkk], w=[KHT])
                            yield
                            bA, kA = bankf()
                            bB, kB = bankf()
                            bC, kC = bankf()
                            bA3, bB3 = v3(bA[:, :], 4), v3(bB[:, :], 4)
                            for c4 in range(4):
                                for e in range(2):
                                    h = h_[e]
                                    P.mm(bA[h, c4 * 128:(c4 + 1) * 128], Bt.ap[h, c4, :], w3(AR)[h, c4, :], r=[Bt, AR], w=[kA])
                                    P.mm(bB[h, c4 * 128:(c4 + 1) * 128], Kt.ap[h, c4, :], w3(AR)[h, c4, :], r=[Kt, AR], w=[kB])
                                    P.mm(bC[h, c4 * 64:(c4 + 1) * 64], AR.ap[h, c4, 0, :], Bt.ap[h, c4, :], r=[Bt, AR], w=[kC])
                            yield
                            bC3 = v3(bC[:, 0:256], 4)
                            P.tt("dve", BA.ap[:, :, 1, :], bA3[:, :, 64:128], mI, ALU.mult, r=[kA, cf], w=[BA])
                            P.tt("dve", AAK.ap, bB3[:, :, 0:64], mS, ALU.mult, r=[kB, cf], w=[AAK])
                            P.tt("dve", AKR.ap, bB3[:, :, 64:128], mI, ALU.mult, r=[kB, cf], w=[AKR])
                            yield
                            up, lo = ("mU", "mL") if d == 0 else ("mL", "mU")
                            P.tt("dve", TT.ap, bA3[:, :, 0:64], mask(up + "0"), ALU.mult, r=[kA, cf], w=[TT])
                            P.tt("dve", TT.ap, TT.ap, idh4, ALU.add, r=[TT, cf], w=[TT])
                            P.tt("dve", TTt.ap, bC3, mask(lo + "0"), ALU.mult, r=[kC, cf], w=[TTt])
                            P.tt("dve", TTt.ap, TTt.ap, idh4, ALU.add, r=[TTt, cf], w=[TTt])
                            yield
                            bX, kX = bankf()
                            bY, kY_ = bankf()
                            bZ, kZ = bankf()
                            P.tt("dve", NM.ap[:, 1, :, :], bC3, mask(lo + "1"), ALU.mult, r=[kC, cf], w=NM.K((1, 2)))
                            for jr in range(1, 6):
                                for c4 in range(4):
                                    for e in range(2):
                                        h = h_[e]
                                        P.mm(bX[h, c4 * 64:(c4 + 1) * 64], NM.ap[h, jr % 2, c4, :], TT.ap[h, c4, :], r=NM.K((jr % 2, jr % 2 + 1)) + TT.K(), w=[kX])
                                if jr < 5:
                                    P.tt("dve", NM.ap[:, (jr + 1) % 2, :, :], bC3, mask(lo + str(jr + 1)), ALU.mult, r=[kC, cf], w=NM.K(((jr + 1) % 2, (jr + 1) % 2 + 1)))
                                yield
                                P.cp("act", XB.ap, v3(bX[:, 0:256], 4), r=[kX], w=[XB])
                                yield
                                for c4 in range(4):
                                    for e in range(2):
                                        h = h_[e]
                                        P.mm(bY[h, c4 * 64:(c4 + 1) * 64], TTt.ap[h, c4, :], XB.ap[h, c4, :], r=[TTt, XB], w=[kY_])
                                        if jr < 5:
                                            P.mm(bZ[h, c4 * 64:(c4 + 1) * 64], XB.ap[h, c4, :], TTt.ap[h, c4, :], r=[TTt, XB], w=[kZ])
                                yield
                                P.tt("dve", TT.ap, TT.ap, v3(bY[:, 0:256], 4), ALU.add, r=[kY_, TT], w=[TT])
                                if jr < 5:
                                    P.tt("dve", TTt.ap, TTt.ap, v3(bZ[:, 0:256], 4), ALU.add, r=[kZ, TTt], w=[TTt])
                            yield
                            for c4 in range(4):
                                ch = sgm * 4 + c4
                                for e in range(2):
                                    h = h_[e]
                                    P.mm(bB[h, c4 * 64:(c4 + 1) * 64], AAK.ap[h, c4, :], VT.ap[h, ch, :], r=AAK.K() + VT.K((ch, ch + 1)), w=[kB])
                            P.cp("act", AXb.ap[:, :, 1, :], v3(bB[:, 0:256], 4), r=[kB], w=[AXb])
                            yield
                            for c4 in range(4):
                                for e in range(2):
                                    h = h_[e]
                                    P.mm(bA[h, c4 * 128:(c4 + 1) * 128], TT.ap[h, c4, :], w3(AXb)[h, c4, :], r=[TT, AXb], w=[kA])
                            P.cp("act", w3(AU), bA3, r=[kA], w=[AU])
                            yield
                            bM, kM = bankf()
                            bM3 = v3(bM[:, :], 4)
                            for c4 in range(4):
                                ch = sgm * 4 + c4
                                for e in range(2):
                                    h = h_[e]
                                    P.mm(bM[h, c4 * 128:(c4 + 1) * 128], AU.ap[h, c4, 0, :], w3(BA)[h, c4, :], r=[AU, BA], w=[kM])
                                    P.mm(bB[h, c4 * 64:(c4 + 1) * 64], BA.ap[h, c4, 0, :], AU.ap[h, c4, 1, :], start=True, stop=False, r=[AU, BA], w=[kB])
                                    P.mm(bB[h, c4 * 64:(c4 + 1) * 64], KHT.ap[h, c4, :], VT.ap[h, ch, :], start=False, stop=True, r=KHT.K() + VT.K((ch, ch + 1)), w=[kB])
                                    P.mm(bC[h, c4 * 64:(c4 + 1) * 64], BA.ap[h, c4, 1, :], AU.ap[h, c4, 1, :], start=True, stop=False, r=[AU, BA], w=[kC])
                                    P.mm(bC[h, c4 * 64:(c4 + 1) * 64], AKR.ap[h, c4, :], VT.ap[h, ch, :], start=False, stop=True, r=AKR.K() + VT.K((ch, ch + 1)), w=[kC])
                            yield
                            chs = (sgm * 4, (sgm + 1) * 4)
                            for c4 in range(4):
                                ch = sgm * 4 + c4
                                P.stt("dve", McT.ap[:, ch, :], idh, WC.ap[:, c4:c4 + 1], bM3[:, c4, 0:64], ALU.mult, ALU.add, r=[kM, cf, WC], w=McT.K((ch, ch + 1)))
                            P.tt("dve", Rh.ap[:, chs[0]:chs[1], :], bM3[:, :, 64:128], AR.ap[:, :, 1, :], ALU.add, r=[kM, AR], w=Rh.K(chs))
                            P.cp("act", Gc.ap[:, chs[0]:chs[1], :], v3(bB[:, 0:256], 4), r=[kB], w=Gc.K(chs))
                            P.cp("act", YvT.ap[:, chs[0]:chs[1], :], v3(bC[:, 0:256], 4), r=[kC], w=YvT.K(chs))


                        def _chain(Tm, bankf, segs):
                            for sg__ in segs:
                                yield from seg_body(Tm, bankf, sg__)

                        _bi = {0: [0], 1: [0]}

                        def _bankf(which):
                            def f():
                                i_ = which * 4 + (3, 3, 0, 1, 2, 3, 0, 1, 3)[_bi[which][0] % 9]
                                _bi[which][0] += 1
                                return P.ps[i_], P.psk[i_]
                            return f

                        ga = _chain(TA, _bankf(0), [0, 2, 4])
                        gb = _chain(TB, _bankf(1), [1, 3, 5])
                        alive = [ga]
                        lead = 1
                        while alive:
                            for g_ in list(alive):
                                try:
                                    next(g_)
                                except StopIteration:
                                    alive.remove(g_)
                            if lead is not None:
                                lead -= 1
                                if lead == 0:
                                    alive.append(gb)
                                    lead = None
                        if lead is not None:
                            for _ in gb:
                                pass
                        P.dma("sp", SI.ap, sinit[l, d, p], "sinit", w=[SI])
                        if d == 1:
                            P.tt("dve", YvT.ap, YvT.ap, YS.ap, ALU.add, r=[YvT, YS], w=[YvT])
                        order = list(range(NCH)) if d == 0 else list(range(NCH - 1, -1, -1))
                        for idx, ch in enumerate(order):
                            sgm = ch // 4
                            first = (ch % 4 == 0) if d == 0 else (ch % 4 == 3)
                            last = (ch % 4 == 3) if d == 0 else (ch % 4 == 0)
                            if first:
                                if d == 0:
                                    rule = "init" if sgm == 0 else ("carry" if sgm in (1, 2, 3) else "zero")
                                else:
                                    rule = "init" if sgm == 3 else ("carry" if sgm in (0, 1, 2) else "zero")
                                if rule == "init":
                                    P.cp("dve", STb.ap, SI.ap, r=[SI], w=[STb])
                                elif rule == "zero":
                                    P.memset("dve", STb.ap, 0.0, w=[STb])
                                else:
                                    P.ts("dve", STb.ap, STb.ap, cm.ap[:, 0:1], None, ALU.mult, r=[STb, cm], w=[STb])
                            bY, kY = P.bank()
                            bS, kS = P.bank()
                            for e in range(2):
                                h = h_[e]
                                P.mm(bS[h, 0:64], McT.ap[h, ch, :], STb.ap[h, :], r=McT.K((ch, ch + 1)) + STb.K(), w=[kS])
                            for e in range(2):
                                h = h_[e]
                                P.mm(bY[h, 0:64], Rh.ap[h, ch, :], STb.ap[h, :], r=Rh.K((ch, ch + 1)) + STb.K(), w=[kY])
                            P.tt("dve", STb.ap, bS[:, 0:64], Gc.ap[:, ch, :], ALU.add, r=[kS] + Gc.K((ch, ch + 1)), w=[STb])
                            if last:
                                P.tt("dve", STO.ap[:, sgm, :], bS[:, 0:64], Gc.ap[:, ch, :], ALU.add, r=[kS] + Gc.K((ch, ch + 1)), w=STO.K((sgm, sgm + 1)))
                            P.tt("dve", YS.ap[:, ch, :], bY[:, 0:64], YvT.ap[:, ch, :], ALU.add, r=[kY] + YvT.K((ch, ch + 1)), w=YS.K((ch, ch + 1)))
                        chk("pass")
                        for half in range(2):
                            bk, bkk = P.bank()
                            for s3 in range(3):
                                sgm = half * 3 + s3
                                P.tr(bk[0:64, s3 * 128:(s3 + 1) * 128], STO.ap[:, sgm, :], ident_f, r=STO.K((sgm, sgm + 1)) + cf.K(), w=[bkk])
                            P.cp("act", STOt.ap[0:64, half * 3:(half + 1) * 3, :], v3(bk[0:64, 0:384], 3), r=[bkk], w=STOt.K((half * 3, (half + 1) * 3)))
                        for sgm in range(NSEG):
                            P.dma("sp", st_out[sgm, l, d, 2 * p:2 * p + 2, :, :].rearrange("e v k -> v e k"),
                                  STOt.ap[0:64, sgm, :].rearrange("v (e k) -> v e k", e=2), "stout", r=STOt.K((sgm, sgm + 1)))

                    chk("stout")
                    if p == 0:
                        P.dump(f"ys{l}", YS)
                    YN = Buf(A.t, P2.off, [NCH, 64], F32)
                    SQb = Buf(A.t, NB.off, [NCH, 64], F32)
                    MEAN = _NS()
                    MEAN.ap = MV.ap[:, 0, :]
                    VAR = _NS()
                    VAR.ap = MV.ap[:, 1, :]
                    P.S.op("dve", lambda e, o=MEAN.ap, i=YS.ap: e.tensor_reduce(out=o, in_=i, axis=AX.X, op=ALU.add), _keys([YS]), _keys([MV]))
                    P.ts("dve", MEAN.ap, MEAN.ap, 1.0 / 64, None, ALU.mult, r=[MV], w=[MV])
                    P.tt("dve", YS.ap, YS.ap, MEAN.ap.unsqueeze(2).to_broadcast([128, NCH, 64]), ALU.subtract, r=[YS, MV], w=[YS])
                    P.act(SQb.ap, YS.ap, AF.Square, r=[YS], w=[SQb])
                    P.S.op("dve", lambda e, o=VAR.ap, i=SQb.ap: e.tensor_reduce(out=o, in_=i, axis=AX.X, op=ALU.add), _keys([SQb]), _keys([MV]))
                    P.act(VAR.ap, VAR.ap, AF.Sqrt, bias=eps_gn, scale=1.0 / 64, r=[MV, cf], w=[MV])
                    P.recip(VAR.ap, VAR.ap, r=[MV], w=[MV])
                    P.tt("dve", YS.ap, YS.ap, VAR.ap.unsqueeze(2).to_broadcast([128, NCH, 64]), ALU.mult, r=[YS, MV], w=[YS])
                    for g3 in range(3):
                        bk, bkk = P.bank()
                        for cc in range(8):
                            ch = g3 * 8 + cc
                            for e in range(2):
                                P.mm(bk[h_[e], cc * 64:(cc + 1) * 64], YS.ap[h_[e], ch, :], ident_f[h_[e], h_[e]], r=YS.K((ch, ch + 1)) + cf.K(), w=[bkk])
                        P.act(YN.flat[:, g3 * 512:(g3 + 1) * 512], bk[:, :], AF.Identity, bias=ppc(l, "lnb", p), scale=ppc(l, "lnw", p),
                              r=[bkk, pp[l]], w=YN.K((g3 * 8, (g3 + 1) * 8)))
                    for tb in range(3):
                        cs = (tb * 512, (tb + 1) * 512)
                        csl = slice(cs[0], cs[1])
                        bk, bkk = P.bank()
                        P.mm(bk[:, :], blk_f, BS.ap[:, csl], r=BS.K(cs) + cf.K(), w=[bkk])
                        P.tt("dve", Pp.ap[:, csl], bk[:, :], V.ap[:, csl], ALU.mult, r=[bkk] + V.K(cs), w=Pp.K(cs))
                        P.tt("dve", YN.flat[:, csl], YN.flat[:, csl], Pp.ap[:, csl], ALU.add, r=YN.K((tb * 8, (tb + 1) * 8)) + Pp.K(cs), w=YN.K((tb * 8, (tb + 1) * 8)))
                        bk2, bkk2 = P.bank()
                        P.mm(bk2[:, :], gup.ap[:, p * 128:(p + 1) * 128], sg.ap[:, csl], r=[gup] + sg.K(cs), w=[bkk2])
                        P.tt("dve", mixedT.ap[:, p, csl], YN.flat[:, csl], bk2[:, :], ALU.mult, r=YN.K((tb * 8, (tb + 1) * 8)) + [bkk2], w=mixedT.K((p, p + 1), cs))
                P.dump(f"mixa{l}", mixedT)
                A.release(m_rw)

                chk("rwkv")
                m_f = A.mark()
                fT = A.alloc([2, T], BF16)
                FC = A.alloc([12, 4, 128], BF16)
                DCb = A.alloc([8, 1024], BF16)
                DSb = A.alloc([8, 1024], BF16)
                DCs = A.alloc([2, 256], BF16)
                DSs = A.alloc([2, 256], BF16)
                zT = A.alloc([2, T], BF16)
                WFb = [A.alloc([128], BF16) for _ in range(2)]
                Pf = A.alloc([T], F32)
                P.dma("sp", DCb.ap, dft_big[0].rearrange("(k p) n -> p k n", p=128), "dft", w=[DCb])
                P.dma("sp", DSb.ap, dft_big[1].rearrange("(k p) n -> p k n", p=128), "dft", w=[DSb])
                P.dma("sp", DCs.ap, dft_sm[0].rearrange("(k p) n -> p k n", p=128), "dft", w=[DCs])
                P.dma("sp", DSs.ap, dft_sm[1].rearrange("(k p) n -> p k n", p=128), "dft", w=[DSs])
                for j in range(2):
                    P.memset("dve", WFb[j].ap, 0.0, w=[WFb[j]])
                    for e in range(2):
                        P.dma("pool", WFb[j].ap[e * 64:(e + 1) * 64, e * 64:(e + 1) * 64], w_fnet[l, 2 * j + e], "wfn", w=[WFb[j]])
                    proj_chunk(l, 15 + j, wsm, Pf)
                    P.cp("dve", fT.ap[:, j, :], Pf.ap, r=[Pf], w=fT.K((j, j + 1)))
                for i in range(12):
                    bk, bkk = P.bank()
                    for j in range(2):
                        P.mm(bk[:, j * 256:(j + 1) * 256], fT.ap[:, j, i * 128:(i + 1) * 128], dftphi, r=fT.K((j, j + 1), (i * 128, (i + 1) * 128)) + cb.K(), w=[bkk])
                    P.cp("act" if i % 2 else "dve", FC.ap[:, i, :, :], v3(bk[:, :], 4), r=[bkk], w=FC.K((i, i + 1)))
                for j in range(2):
                    for hb in range(2):
                        bk, bkk = P.bank()
                        n = 0
                        for lt in range(8):
                            for cs_, Dm in ((0, DCb), (1, DSb)):
                                P.mm(bk[:, :], FC.ap[:, lt, 2 * j + cs_, :], Dm.ap[:, lt, hb * 512:(hb + 1) * 512], start=(n == 0), stop=(n == 15),
                                     r=FC.K((lt, lt + 1)) + Dm.K((lt, lt + 1)), w=[bkk])
                                n += 1
                        P.cp("act", zT.ap[:, j, hb * 512:(hb + 1) * 512], bk[:, :], r=[bkk], w=zT.K((j, j + 1), (hb * 512, (hb + 1) * 512)))
                    for s45 in range(2):
                        bk, bkk = P.bank()
                        n = 0
                        for lt2 in range(2):
                            lt = 8 + 2 * s45 + lt2
                            for cs_, Dm in ((0, DCs), (1, DSs)):
                                P.mm(bk[:, 0:256], FC.ap[:, lt, 2 * j + cs_, :], Dm.ap[:, lt2, :], start=(n == 0), stop=(n == 3),
                                     r=FC.K((lt, lt + 1)) + Dm.K(), w=[bkk])
                                n += 1
                        t0 = 1024 + 256 * s45
                        P.cp("dve", zT.ap[:, j, t0:t0 + 256], bk[:, 0:256], r=[bkk], w=zT.K((j, j + 1), (t0, t0 + 256)))
                    for tb in range(3):
                        cs = (tb * 512, (tb + 1) * 512)
                        bk, bkk = P.bank()
                        P.mm(bk[:, :], WFb[j].ap, zT.ap[:, j, cs[0]:cs[1]], r=WFb[j].K() + zT.K((j, j + 1), cs), w=[bkk])
                        P.cp("act", mixedT.ap[:, 4 + j, cs[0]:cs[1]], bk[:, :], r=[bkk], w=mixedT.K((4 + j, 5 + j), cs))
                A.release(m_f)

                chk("fnet")
                m_p = A.mark()
                PTM = A.alloc([12, 256], BF16)
                pT = A.alloc([2, T], F32)
                PMb = [A.alloc([8, 1024], BF16)] * 2
                PMs = A.alloc([4, 2, 256], BF16)
                ICN = A.alloc([2, T], F32)
                dT = A.alloc([2, T], BF16)
                WPc = A.alloc([8, 256], BF16)
                WPb = [A.alloc([128], BF16) for _ in range(2)]
                PD = A.alloc([T], F32)
                load_w(WPc, w_in[l][:, 2176:2432], "wpc")
                P.dma("sp", ICN.ap, icnt_d, "icn", w=[ICN])
                for g in range(4):
                    P.dma("sp", PMs.ap[:, g, :, :], pm_sm[g].rearrange("(k p) n -> p k n", p=128), "pms", w=[PMs])
                for j in range(2):
                    P.memset("dve", WPb[j].ap, 0.0, w=[WPb[j]])
                    for e in range(2):
                        P.dma("pool", WPb[j].ap[e * 64:(e + 1) * 64, e * 64:(e + 1) * 64], w_pool[l, 2 * j + e], "wpl", w=[WPb[j]])
                    proj_chunk(l, 17 + j, wsm, PD)
                    P.cp("dve", pT.ap[:, j, :], PD.ap, r=[PD], w=pT.K((j, j + 1)))
                for i in range(12):
                    bk, bkk = P.bank()
                    for kc in range(8):
                        P.mm(bk[:, 0:256], hT.ap[:, kc, i * 128:(i + 1) * 128], WPc.ap[:, kc, :], start=(kc == 0), stop=(kc == 7),
                             r=hT.K((kc, kc + 1), (i * 128, (i + 1) * 128)) + WPc.K(), w=[bkk])
                    P.cp("act" if i % 2 else "dve", PTM.ap[:, i, :], bk[:, 0:256], r=[bkk], w=PTM.K((i, i + 1)))
                for g in range(4):
                    j, e = g // 2, g % 2
                    h = slice(e * 64, (e + 1) * 64)
                    pmb = PMb[g % 2]
                    P.dma("sp", pmb.ap, pm_big[g].rearrange("(k p) n -> p k n", p=128), "pmb0", w=[pmb])
                    for hb in range(2):
                        bk, bkk = P.bank()
                        for lt in range(8):
                            P.mm(bk[h, :], PTM.ap[:, lt, g * 64:(g + 1) * 64], pmb.ap[:, lt, hb * 512:(hb + 1) * 512], start=(lt == 0), stop=(lt == 7),
                                 r=PTM.K((lt, lt + 1)) + pmb.K((lt, lt + 1)), w=[bkk])
                        cs = (hb * 512, (hb + 1) * 512)
                        P.tt("dve", PD.ap[h, cs[0]:cs[1]], bk[h, :], ICN.ap[h, j, cs[0]:cs[1]], ALU.mult, r=[bkk] + ICN.K((j, j + 1), cs), w=PD.K(cs))
                    bk, bkk = P.bank()
                    for s45 in range(2):
                        for lt2 in range(2):
                            lt = 8 + 2 * s45 + lt2
                            P.mm(bk[h, s45 * 256:(s45 + 1) * 256], PTM.ap[:, lt, g * 64:(g + 1) * 64], PMs.ap[:, g, lt2, :], start=(lt2 == 0), stop=(lt2 == 1),
                                 r=PTM.K((lt, lt + 1)) + PMs.K((g, g + 1)), w=[bkk])
                    P.tt("dve", PD.ap[h, 1024:1536], bk[h, :], ICN.ap[h, j, 1024:1536], ALU.mult, r=[bkk] + ICN.K((j, j + 1), (1024, 1536)), w=PD.K((1024, 1536)))
                    if e == 1:
                        P.tt("dve", dT.ap[:, j, :], PD.ap, pT.ap[:, j, :], ALU.subtract, r=PD.K() + pT.K((j, j + 1)), w=dT.K((j, j + 1)))
                        for tb in range(3):
                            cs = (tb * 512, (tb + 1) * 512)
                            bk2, bkk2 = P.bank()
                            P.mm(bk2[:, :], WPb[j].ap, dT.ap[:, j, cs[0]:cs[1]], r=WPb[j].K() + dT.K((j, j + 1), cs), w=[bkk2])
                            P.act(mixedT.ap[:, 6 + j, cs[0]:cs[1]], bk2[:, :], AF.Identity, scale=ppc(l, "psc", j), r=[bkk2, pp[l]], w=mixedT.K((6 + j, 7 + j), cs))
                A.release(m_p)
                P.dump(f"mix{l}", mixedT)

                chk("pool")
                for mo in range(8):
                    wb = wsm[wcnt[0] % 2]
                    load_w(wb, w_out[l][:, mo * 128:(mo + 1) * 128], f"wsm{wcnt[0] % 2}")
                    wcnt[0] += 1
                    for tb in range(3):
                        cs = (tb * 512, (tb + 1) * 512)
                        bk, bkk = P.bank()
                        for kc in range(8):
                            P.mm(bk[:, :], wb.ap[:, kc, :], mixedT.ap[:, kc, cs[0]:cs[1]], start=(kc == 0), stop=(kc == 7),
                                 r=wb.K() + mixedT.K((kc, kc + 1), cs), w=[bkk])
                        for s2 in range(2):
                            sg_ = tb * 2 + s2
                            c2 = (sg_ * 256, (sg_ + 1) * 256)
                            P.stt("dve", xT.ap[:, mo, c2[0]:c2[1]], bk[:, s2 * 256:(s2 + 1) * 256], modT.ap[:, 16 + mo, sg_:sg_ + 1], xT.ap[:, mo, c2[0]:c2[1]],
                                  ALU.mult, ALU.add, r=[bkk] + modT.K() + xT.K((mo, mo + 1), c2), w=xT.K((mo, mo + 1), c2))
                A.release(m_mix)
                P.dump(f"xmid{l}", xT)

                chk("wout")
                run(rmsnorm(l, A2, 24))
                m_ff = A.mark()
                hid = A.alloc([32, T], BF16)
                w1b = [A.alloc([8, 128], BF16) for _ in range(2)]
                w2b = [A.alloc([32, 128], BF16) for _ in range(2)]
                rl = [A.alloc([512], F32) for _ in range(2)]
                for fo in range(32):
                    wb = w1b[fo % 2]
                    load_w(wb, w_ff1[l][:, fo * 128:(fo + 1) * 128], f"w1b{fo % 2}")
                    for tb in range(3):
                        cs = (tb * 512, (tb + 1) * 512)
                        bk, bkk = P.bank()
                        for kc in range(8):
                            P.mm(bk[:, :], wb.ap[:, kc, :], hT.ap[:, kc, cs[0]:cs[1]], start=(kc == 0), stop=(kc == 7),
                                 r=wb.K() + hT.K((kc, kc + 1), cs), w=[bkk])
                        rb = rl[(fo * 3 + tb) % 2]
                        P.act(rb.ap, bk[:, :], AF.Relu, r=[bkk], w=[rb])
                        P.tt("dve", hid.ap[:, fo, cs[0]:cs[1]], rb.ap, rb.ap, ALU.mult, r=[rb], w=hid.K((fo, fo + 1), cs))
                for mo in range(8):
                    wb = w2b[mo % 2]
                    load_w(wb, w_ff2[l][:, mo * 128:(mo + 1) * 128], f"w2b{mo % 2}")
                    for tb in range(3):
                        cs = (tb * 512, (tb + 1) * 512)
                        bk, bkk = P.bank()
                        for kc in range(32):
                            P.mm(bk[:, :], wb.ap[:, kc, :], hid.ap[:, kc, cs[0]:cs[1]], start=(kc == 0), stop=(kc == 31),
                                 r=wb.K((kc, kc + 1)) + hid.K((kc, kc + 1), cs), w=[bkk])
                        for s2 in range(2):
                            sg_ = tb * 2 + s2
                            c2 = (sg_ * 256, (sg_ + 1) * 256)
                            P.stt("dve", xT.ap[:, mo, c2[0]:c2[1]], bk[:, s2 * 256:(s2 + 1) * 256], modT.ap[:, 40 + mo, sg_:sg_ + 1], xT.ap[:, mo, c2[0]:c2[1]],
                                  ALU.mult, ALU.add, r=[bkk] + modT.K() + xT.K((mo, mo + 1), c2), w=xT.K((mo, mo + 1), c2))
                A.release(m_ff)
                P.dump(f"x{l}", xT)

            chk("layers")
            m_fin = A.mark()
            yn = A.alloc([8, 512], F32)
            yo = [A.alloc([D], F32) for _ in range(2)]
            for tb in rmsnorm(DEPTH - 1, None, 0, final=True, yn=yn):
                for it in range(4):
                    i = tb * 4 + it
                    ob = yo[i % 2]
                    for q in range(2):
                        bk, bkk = P.bank()
                        for cc in range(4):
                            c = q * 4 + cc
                            P.tr(bk[:, cc * 128:(cc + 1) * 128], yn.ap[:, c, it * 128:(it + 1) * 128], ident_f, r=yn.K((c, c + 1)) + cf.K(), w=[bkk])
                        P.cp("act" if q == 0 else "dve", ob.ap[:, q * 512:(q + 1) * 512], bk[:, :], r=[bkk], w=ob.K((q * 512, (q + 1) * 512)))
                    P.dma("sp", y_out[i * 128:(i + 1) * 128, :], ob.ap, f"yout{i % 2}", r=[ob])
            A.release(m_fin)
        except _Stop:
            pass
        P.S.final_wait("sp")
        P.S.emit(nc, st)
    return nc


POOL_WINDOWS = (2, 4, 8, 16)


def _win(n, w):
    t = np.arange(n)
    lo = np.clip(t - w // 2, 0, n)
    hi = np.clip(t + w - w // 2, 0, n)
    return lo, hi


def _consts():
    cf = np.zeros((128, NCF), np.float32)
    cf[:, CF["ident"]:CF["ident"] + 128] = np.eye(128)
    blk = np.zeros((128, 128), np.float32)
    blk[:64, :64] = 1
    blk[64:, 64:] = 1
    cf[:, CF["blk"]:CF["blk"] + 128] = blk
    cf[:, CF["ones"]:CF["ones"] + 128] = 1
    s = np.arange(128)[:, None] % 64
    t = np.arange(64)[None, :]
    cf[:, CF["mSU"]:CF["mSU"] + 64] = s < t
    cf[:, CF["mIU"]:CF["mIU"] + 64] = s <= t
    cf[:, CF["mSL"]:CF["mSL"] + 64] = s > t
    cf[:, CF["mIL"]:CF["mIL"] + 64] = s >= t
    cf[:, CF["idh"]:CF["idh"] + 64] = s == t
    for j in range(6):
        bs = 1 << j
        same_big = (s // (2 * bs)) == (t // (2 * bs))
        diff_small = (s // bs) != (t // bs)
        cf[:, CF["mU%d" % j]:CF["mU%d" % j] + 64] = same_big & diff_small & (s < t)
        cf[:, CF["mL%d" % j]:CF["mL%d" % j] + 64] = same_big & diff_small & (s > t)
    csm = np.ones((128, 256), np.float32)
    csm[:, ::64] = 0
    cf[:, CF["csm"]:CF["csm"] + 256] = csm
    cf[:, CF["eps"]] = 1e-6
    cf[:, CF["eps"] + 1] = 64e-5
    cf[:, CF["eps"] + 2] = 0.5 * C0
    cf[:, CF["eps"] + 3] = 0.5
    cb = np.zeros((128, 384), np.float32)
    cb[:, 0:128] = np.eye(128)
    cc = np.arange(64)
    ang = 2 * np.pi * np.outer(cc, cc) / 64
    for e in range(2):
        cb[e * 64:(e + 1) * 64, 128 + e * 64:128 + (e + 1) * 64] = np.cos(ang)
        cb[e * 64:(e + 1) * 64, 256 + e * 64:256 + (e + 1) * 64] = np.sin(ang)
    return cf, cb.astype(ml_dtypes.bfloat16)


def _dft(L):
    ll = np.arange(L)
    ang = 2 * np.pi * np.outer(ll, ll) / L
    sc = 1.0 / math.sqrt(L * 64)
    return np.cos(ang) * sc, -np.sin(ang) * sc


def _pool_1d(L):
    ms, cnts = [], []
    for w in POOL_WINDOWS:
        lo, hi = _win(L, w)
        l = np.arange(L)[:, None]
        m = ((l >= lo[None, :]) & (l < hi[None, :])).astype(np.float32)
        ms.append(m)
        cnts.append((hi - lo).astype(np.float32))
    return ms, cnts


def _pool_2d(rows, cols):
    ms, cnts = [], []
    for w in POOL_WINDOWS:
        lor, hir = _win(rows, w)
        loc, hic = _win(cols, w)
        r = np.arange(rows)[:, None]
        c = np.arange(cols)[:, None]
        mr = ((r >= lor[None, :]) & (r < hir[None, :])).astype(np.float32)
        mc = ((c >= loc[None, :]) & (c < hic[None, :])).astype(np.float32)
        m = np.einsum("ab,cd->acbd", mr, mc).reshape(rows * cols, rows * cols)
        cnt = np.outer((hir - lor), (hic - loc)).reshape(-1).astype(np.float32)
        ms.append(m)
        cnts.append(cnt)
    return ms, cnts


def _core_consts(is_sample):
    bf = ml_dtypes.bfloat16
    c256, s256 = _dft(256)
    dft_sm = np.stack([c256, s256]).astype(bf)
    m1, n1 = _pool_1d(256)
    pm_sm = np.stack(m1).astype(bf)
    if is_sample:
        c, s = _dft(1024)
        dft_big = np.stack([c, s]).astype(bf)
        m2, n2 = _pool_2d(16, 64)
        pm_big = np.stack(m2).astype(bf)
        cnt_big = n2
    else:
        dft_big = np.zeros((2, 1024, 1024), np.float32)
        pm_big = np.zeros((4, 1024, 1024), np.float32)
        for b in range(4):
            sl = slice(b * 256, (b + 1) * 256)
            dft_big[0, sl, sl] = c256
            dft_big[1, sl, sl] = s256
            for g in range(4):
                pm_big[g, sl, sl] = m1[g]
        dft_big = dft_big.astype(bf)
        pm_big = pm_big.astype(bf)
        cnt_big = [np.tile(n1[g], 4) for g in range(4)]
    icnt = np.zeros((128, 2, T), np.float32)
    for g in range(4):
        cnt = np.concatenate([cnt_big[g], n1[g], n1[g]])
        icnt[(g % 2) * 64:(g % 2 + 1) * 64, g // 2, :] = (1.0 / cnt)[None, :]
    return dict(dft_big=dft_big, dft_sm=dft_sm, pm_big=pm_big, pm_sm=pm_sm, icnt=icnt)


def _colmajor(v, n):
    return np.ascontiguousarray(np.asarray(v, np.float32).reshape(n, 128).T)


def _pack_pp(inp, l):
    pp = np.zeros((128, NPP), np.float32)
    pp[:, PP["n1g"]:PP["n1g"] + 8] = _colmajor(inp["norm1_g"][l], 8)
    pp[:, PP["mu"]:PP["mu"] + 15] = _colmajor(inp["mu_shift"][l], 15)
    pp[:, PP["w0"]:PP["w0"] + 8] = _colmajor(inp["w0"][l].reshape(-1), 8)
    pp[:, PP["a0"]:PP["a0"] + 8] = _colmajor(inp["a0"][l].reshape(-1), 8)
    pp[:, PP["kk"]:PP["kk"] + 4] = _colmajor(inp["k_k"][l], 4)
    pp[:, PP["ka"]:PP["ka"] + 4] = _colmajor(inp["k_a"][l], 4)
    pp[:, PP["rk"]:PP["rk"] + 4] = _colmajor(inp["r_k"][l].reshape(-1), 4)
    pp[:, PP["lnw"]:PP["lnw"] + 4] = _colmajor(inp["ln_x_w"][l], 4)
    pp[:, PP["lnb"]:PP["lnb"] + 4] = _colmajor(inp["ln_x_b"][l], 4)
    pp[:, PP["psc"]:PP["psc"] + 2] = _colmajor(inp["pool_scale"][l], 2)
    pp[:, PP["n2g"]:PP["n2g"] + 8] = _colmajor(inp["norm2_g"][l], 8)
    pp[:, PP["bada"]:PP["bada"] + 48] = _colmajor(inp["b_ada"][l], 48)
    pp[:, PP["fng"]:PP["fng"] + 8] = _colmajor(inp["final_norm_g"], 8)
    return pp


_NC_CACHE = {}


def _prep_inputs(inp):
    inp = {k: np.asarray(v) for k, v in inp.items()}
    cf, cb = _consts()
    pp = np.stack([_pack_pp(inp, l) for l in range(DEPTH)])
    shared = dict(
        pp=pp, cf=cf, cb=cb,
        w_ada=np.ascontiguousarray(inp["w_ada"], np.float32), w_in=np.ascontiguousarray(inp["w_in"], np.float32),
        w_up=np.ascontiguousarray(inp["w_up"].reshape(DEPTH, 128, 512), np.float32),
        a_up=np.ascontiguousarray(inp["a_up"].reshape(DEPTH, 128, 512), np.float32),
        g_up=np.ascontiguousarray(inp["g_up"], np.float32),
        w_fnet=np.ascontiguousarray(inp["w_fnet"], np.float32), w_pool=np.ascontiguousarray(inp["w_pool"], np.float32),
        w_out=np.ascontiguousarray(inp["w_out"], np.float32), w_ff1=np.ascontiguousarray(inp["w_ff1"], np.float32),
        w_ff2=np.ascontiguousarray(inp["w_ff2"], np.float32),
    )
    cc = {True: _core_consts(True), False: _core_consts(False)}
    xp, xs = inp["x_prompt"], inp["x_sample"]
    in_maps, plan = [], []
    for i in range(8):
        if i < 4:
            pb = [2 * i, 2 * i + 1]
            x = np.concatenate([xs[i], xp[pb[0]], xp[pb[1]]], 0)
            cond = np.stack([inp["c"][i]] * 4 + [inp["c_ctx"]] * 2, 1)
            sw = inp["state_wkv"][i]
            si = np.transpose(sw, (0, 1, 2, 4, 3)).reshape(DEPTH, 2, 4, 128, 64)
            cm = 1.0
            interior = [1, 2, 3]
            plan.append((i, pb, [4, 5]))
        else:
            j = i - 4
            pb = list(range(8 + 6 * j, 8 + 6 * j + 6))
            x = np.concatenate([xp[b] for b in pb], 0)
            cond = np.stack([inp["c_ctx"]] * 6, 1)
            si = np.zeros((DEPTH, 2, 4, 128, 64), np.float32)
            cm = 0.0
            interior = []
            plan.append((None, pb, list(range(6))))
        cmlr = np.zeros((128, 16), np.float32)
        cmlr[:, 0] = cm
        for b in range(1, NSEG):
            v = 1.0 if b in interior else 0.0
            cmlr[:, b] = v
            cmlr[:, 5 + b] = v
        m = dict(shared)
        m.update(cc[i < 4])
        m.update(x_in=np.ascontiguousarray(x, np.float32), cond=np.ascontiguousarray(cond, np.float32),
                 sinit=np.ascontiguousarray(si, np.float32), cmlr=cmlr)
        in_maps.append(m)
    return in_maps, plan


def kernel(**inp):
    debug = tuple(inp.pop("_debug", ()))
    in_maps, plan = _prep_inputs(inp)
    key = debug
    if key not in _NC_CACHE:
        _NC_CACHE[key] = build_program(debug)
    nc = _NC_CACHE[key]
    res = run_bass_kernel_spmd(nc, in_maps, core_ids=list(range(8)))
    y_prompt = np.zeros((32, 256, D), np.float32)
    y_sample = np.zeros((4, 1024, D), np.float32)
    new_state = np.zeros((32, DEPTH, 2, 8, 64, 64), np.float32)
    for i, (sb, pb, segs) in enumerate(plan):
        r = res.results[i]
        y = np.asarray(r["y_out"])
        so = np.asarray(r["st_out"])
        if sb is not None:
            y_sample[sb] = y[0:1024]
        for b, sgm in zip(pb, segs):
            y_prompt[b] = y[sgm * 256:(sgm + 1) * 256]
            new_state[b] = so[sgm]
    if debug:
        kernel.last_debug = [{k: np.asarray(v) for k, v in r.items() if k.startswith("dbg_")} for r in res.results]
    return (y_prompt, y_sample, new_state)
```
